# Optimizing a Trainium2 kernel written in Bass

```python
import math
import jax, jax.numpy as jnp
from jax import lax
import numpy as np

D_MODEL = 2048
BATCH = 4
SEQ = 4096
DEPTH = 1

HEAD_DIM_A = D_MODEL // 16
ATTN_GROUPS = ((128, 1), (512, 4), (2048, 16))
N_GROUPS = len(ATTN_GROUPS)
HEADS_PER_GROUP = 4
N_HEADS_A = N_GROUPS * HEADS_PER_GROUP
WIDTH_A = N_HEADS_A * HEAD_DIM_A
OUT_WIDTH_A = HEADS_PER_GROUP * HEAD_DIM_A
BLK = 64
N_BUCKETS = 32
MAX_DISTANCE = 1024
LRU_WIDTH = 3 * D_MODEL // 4
LRU_BLOCKS = 12
LRU_BW = LRU_WIDTH // LRU_BLOCKS
CONV_WIDTH = 4
LRU_C = 8.0
N_MEM = 256
MEM_HEADS = 4
MEM_HEAD_DIM = D_MODEL // 8
MEM_WIDTH = MEM_HEADS * MEM_HEAD_DIM
D_FF = 4 * D_MODEL
N_BRANCH = 3
EPS = 1e-6

N_IN = 3 * WIDTH_A + 2 * LRU_WIDTH + MEM_WIDTH
IN_SPLITS = (WIDTH_A, 2 * WIDTH_A, 3 * WIDTH_A, 3 * WIDTH_A + LRU_WIDTH, 3 * WIDTH_A + 2 * LRU_WIDTH)

kernel_name = "hybrid_dilated_rglru_memxattn_block"


def rms_norm(t, gain):
    tf = t.astype(jnp.float32)
    y = tf * lax.rsqrt(jnp.mean(tf * tf, axis=-1, keepdims=True) + EPS) * gain.astype(jnp.float32)
    return y.astype(t.dtype)


def t5_bucket(rel):
    nb = N_BUCKETS // 2
    max_exact = nb // 2
    sign = (rel > 0).astype(np.int32) * nb
    n = np.abs(rel)
    large = max_exact + (np.log(np.maximum(n, 1) / max_exact)
                         / np.log(MAX_DISTANCE / max_exact) * (nb - max_exact)).astype(np.int32)
    large = np.minimum(large, nb - 1)
    return (sign + np.where(n < max_exact, n, large)).astype(np.int32)


def dilated_window_attention(q, k, v, bias_table, window, dilation):
    B, S, H, C = q.shape
    d = dilation
    L = S // d
    radius = window // (2 * d)
    nblk = -(-L // BLK)
    Lp = nblk * BLK
    pad = Lp - L

    def to_strided(t):
        return t.reshape(B, L, d, H, C).transpose(0, 2, 1, 3, 4)

    qs, ks, vs = to_strided(q), to_strided(k), to_strided(v)
    qb = jnp.pad(qs, ((0, 0), (0, 0), (0, pad), (0, 0), (0, 0))).reshape(B, d, nblk, BLK, H, C)

    def windows(t):
        tp = jnp.pad(t, ((0, 0), (0, 0), (BLK, pad + BLK), (0, 0), (0, 0)))
        tb = tp.reshape(B, d, nblk + 2, BLK, H, C)
        return jnp.concatenate([tb[:, :, :-2], tb[:, :, 1:-1], tb[:, :, 2:]], axis=3)

    kw, vw = windows(ks), windows(vs)

    qq = np.arange(BLK)[:, None]
    kk = np.arange(3 * BLK)[None, :]
    rel = kk - BLK - qq
    band = np.abs(rel) <= radius
    key_pos = (np.arange(nblk)[:, None, None] - 1) * BLK + kk[None]
    valid = band[None] & (key_pos >= 0) & (key_pos < L)
    bias = bias_table.astype(jnp.float32)[t5_bucket(rel * d)]
    bias = jnp.transpose(bias, (2, 0, 1))

    scale = 1.0 / math.sqrt(C)
    logits = jnp.einsum('brnqhc,brnkhc->brnhqk', qb, kw).astype(jnp.float32) * scale
    logits = logits + bias[None, None, None]
    logits = jnp.where(valid[None, None, :, None], logits, -1e30)
    m = jnp.max(logits, axis=-1, keepdims=True)
    p = jnp.exp(logits - m)
    s = jnp.sum(p, axis=-1, keepdims=True)
    o = jnp.einsum('brnhqk,brnkhc->brnqhc', p.astype(vw.dtype), vw).astype(jnp.float32)
    o = o / jnp.swapaxes(s, 3, 4)
    lse = jnp.swapaxes((m + jnp.log(s))[..., 0], 3, 4)

    o = o.reshape(B, d, Lp, H, C)[:, :, :L].transpose(0, 2, 1, 3, 4).reshape(B, S, H, C)
    lse = lse.reshape(B, d, Lp, H)[:, :, :L].transpose(0, 2, 1, 3).reshape(B, S, H)
    return o, lse


def _lin_combine(left, right):
    a1, b1 = left
    a2, b2 = right
    return a1 * a2, a2 * b1 + b2


def rg_lru_forward(xc, wa, ba, wi, bi, lam):
    B, S, W = xc.shape
    xb = xc.reshape(B, S, LRU_BLOCKS, LRU_BW)
    r = jax.nn.sigmoid(jnp.einsum('bsnc,ncd->bsnd', xb, wa.astype(jnp.float32)) + ba.astype(jnp.float32)).reshape(B, S, W)
    i = jax.nn.sigmoid(jnp.einsum('bsnc,ncd->bsnd', xb, wi.astype(jnp.float32)) + bi.astype(jnp.float32)).reshape(B, S, W)
    log_a = -LRU_C * jax.nn.softplus(-lam.astype(jnp.float32)) * r
    a = jnp.exp(log_a)
    is_start = (jnp.arange(S) == 0)[None, :, None]
    mult = jnp.where(is_start, 1.0, jnp.sqrt(-jnp.expm1(2.0 * log_a)))
    b = mult * (i * xc)
    _, h = lax.associative_scan(_lin_combine, (a, b), axis=1)
    return h


def setup_inputs(seed: int = 0) -> dict:
    key = jax.random.key(seed)
    ks = jax.random.split(key, 32)
    f32 = jnp.float32

    def dense(k, shape, fan_in):
        return jax.random.normal(k, shape, f32) * (fan_in ** -0.5)

    def gain(k, shape):
        return 1.0 + 0.02 * jax.random.normal(k, shape, f32)

    def small(k, shape):
        return 0.01 * jax.random.normal(k, shape, f32)

    u = jax.random.uniform(ks[14], (DEPTH, 2, LRU_WIDTH), f32, 0.9, 0.999)
    a_base = u ** (1.0 / LRU_C)
    lru_lambda = jnp.log(a_base) - jnp.log1p(-a_base)

    return {
        "x": jax.random.normal(ks[0], (BATCH, SEQ, D_MODEL), f32),
        "mem": jax.random.normal(ks[1], (BATCH, N_MEM, D_MODEL), f32),
        "rel_bias": 0.1 * jax.random.normal(ks[2], (N_BUCKETS, N_HEADS_A), f32),
        "norm_mix": gain(ks[3], (DEPTH, D_MODEL)),
        "norm_mem": gain(ks[4], (DEPTH, D_MODEL)),
        "norm_mlp": gain(ks[5], (DEPTH, D_MODEL)),
        "norm_final": gain(ks[6], (D_MODEL,)),
        "w_in": dense(ks[7], (DEPTH, D_MODEL, N_IN), D_MODEL),
        "w_gate": dense(ks[8], (DEPTH, D_MODEL, N_BRANCH * D_MODEL), D_MODEL),
        "b_gate": small(ks[9], (DEPTH, N_BRANCH * D_MODEL)),
        "conv_w": dense(ks[10], (DEPTH, CONV_WIDTH, LRU_WIDTH), CONV_WIDTH),
        "conv_b": small(ks[11], (DEPTH, LRU_WIDTH)),
        "lru_wa": dense(ks[12], (DEPTH, 2, LRU_BLOCKS, LRU_BW, LRU_BW), LRU_BW),
        "lru_ba": small(ks[13], (DEPTH, 2, LRU_BLOCKS, LRU_BW)),
        "lru_wi": dense(ks[15], (DEPTH, 2, LRU_BLOCKS, LRU_BW, LRU_BW), LRU_BW),
        "lru_bi": small(ks[16], (DEPTH, 2, LRU_BLOCKS, LRU_BW)),
        "lru_lambda": lru_lambda,
        "w_mem_kv": dense(ks[17], (DEPTH, D_MODEL, 2 * MEM_WIDTH), D_MODEL),
        "w_o_attn": dense(ks[18], (DEPTH, OUT_WIDTH_A, D_MODEL), OUT_WIDTH_A),
        "w_o_lru": dense(ks[19], (DEPTH, LRU_WIDTH, D_MODEL), LRU_WIDTH),
        "w_o_mem": dense(ks[20], (DEPTH, MEM_WIDTH, D_MODEL), MEM_WIDTH),
        "w_out": dense(ks[21], (DEPTH, D_MODEL, D_MODEL), D_MODEL),
        "w_up": dense(ks[22], (DEPTH, D_MODEL, D_FF), D_MODEL),
        "w_down": dense(ks[23], (DEPTH, D_FF, D_MODEL), D_FF),
    }


def reference(x, mem, rel_bias, norm_mix, norm_mem, norm_mlp, norm_final, w_in, w_gate, b_gate,
              conv_w, conv_b, lru_wa, lru_ba, lru_wi, lru_bi, lru_lambda, w_mem_kv,
              w_o_attn, w_o_lru, w_o_mem, w_out, w_up, w_down):
    B, S, _ = x.shape
    for l in range(DEPTH):
        h = rms_norm(x, norm_mix[l])
        proj = h @ w_in[l]
        q_a, k_a, v_a, x_b, y_b, q_c = jnp.split(proj, IN_SPLITS, axis=-1)

        q_a = q_a.reshape(B, S, N_HEADS_A, HEAD_DIM_A)
        k_a = k_a.reshape(B, S, N_HEADS_A, HEAD_DIM_A)
        v_a = v_a.reshape(B, S, N_HEADS_A, HEAD_DIM_A)
        outs, lses = [], []
        for g, (window, dil) in enumerate(ATTN_GROUPS):
            hs = slice(g * HEADS_PER_GROUP, (g + 1) * HEADS_PER_GROUP)
            o, lse = dilated_window_attention(q_a[:, :, hs], k_a[:, :, hs], v_a[:, :, hs],
                                              rel_bias[:, hs], window, dil)
            outs.append(o)
            lses.append(lse)
        wts = jax.nn.softmax(jnp.stack(lses), axis=0)
        y_a = jnp.einsum('gbsh,gbshc->bshc', wts, jnp.stack(outs))
        y_a = y_a.reshape(B, S, OUT_WIDTH_A).astype(h.dtype) @ w_o_attn[l]

        kern = conv_w[l].reshape(CONV_WIDTH, 1, LRU_WIDTH).astype(x_b.dtype)
        xc = lax.conv_general_dilated(x_b, kern, window_strides=(1,), padding=[(1, 2)],
                                      dimension_numbers=('NWC', 'WIO', 'NWC'),
                                      feature_group_count=LRU_WIDTH) + conv_b[l]
        xc = xc.astype(jnp.float32)
        h_fwd = rg_lru_forward(xc, lru_wa[l, 0], lru_ba[l, 0], lru_wi[l, 0], lru_bi[l, 0], lru_lambda[l, 0])
        h_bwd = jnp.flip(rg_lru_forward(jnp.flip(xc, axis=1), lru_wa[l, 1], lru_ba[l, 1],
                                        lru_wi[l, 1], lru_bi[l, 1], lru_lambda[l, 1]), axis=1)
        y_lru = (h_fwd + h_bwd).astype(h.dtype) * jax.nn.gelu(y_b)
        y_lru = y_lru @ w_o_lru[l]

        mem_n = rms_norm(mem, norm_mem[l])
        k_c, v_c = jnp.split(mem_n @ w_mem_kv[l], 2, axis=-1)
        q_c = q_c.reshape(B, S, MEM_HEADS, MEM_HEAD_DIM)
        k_c = k_c.reshape(B, N_MEM, MEM_HEADS, MEM_HEAD_DIM)
        v_c = v_c.reshape(B, N_MEM, MEM_HEADS, MEM_HEAD_DIM)
        logits_c = jnp.einsum('bshc,bmhc->bhsm', q_c, k_c).astype(jnp.float32) * (1.0 / math.sqrt(MEM_HEAD_DIM))
        p_c = jax.nn.softmax(logits_c, axis=-1)
        y_c = jnp.einsum('bhsm,bmhc->bshc', p_c.astype(v_c.dtype), v_c).reshape(B, S, MEM_WIDTH)
        y_c = y_c @ w_o_mem[l]

        gates = jax.nn.sigmoid((h @ w_gate[l] + b_gate[l]).astype(jnp.float32)).astype(h.dtype)
        g_a, g_b, g_c = jnp.split(gates, N_BRANCH, axis=-1)
        mixed = g_a * y_a + g_b * y_lru + g_c * y_c
        x = x + mixed @ w_out[l]

        h2 = rms_norm(x, norm_mlp[l])
        x = x + jnp.square(jax.nn.relu(h2 @ w_up[l])) @ w_down[l]

    return rms_norm(x, norm_final)
```

```python
import math
from contextlib import ExitStack

import numpy as np
import concourse.bass as bass
import concourse.mybir as mybir
from concourse.bass_utils import run_bass_kernel_spmd

F32 = mybir.dt.float32
BF16 = mybir.dt.bfloat16
AF = mybir.ActivationFunctionType
ALU = mybir.AluOpType

D = 2048
S = 4096
OWN = 2048
NT = 512
EPS = 1e-6
N_IN = 8704
DEBUG = False

SB_BASE = 16512
SB_END = 229344


class Buf:
    __slots__ = ("name", "writers", "readers", "sem", "semcnt", "excl")

    def __init__(self, name, excl=False):
        self.name = name
        self.excl = excl
        self.writers = {}
        self.readers = {}
        self.sem = None
        self.semcnt = 0


class Op:
    __slots__ = ("eng", "fn", "deps", "signal", "semval", "dma", "sembuf", "n")


class Tens:
    def __init__(self, t, buf):
        self.t = t
        self.buf = buf


ENGS = ("pe", "act", "dve", "pool", "sp")


class Prog:
    def __init__(self, nc):
        self.nc = nc
        self.ops = {e: [] for e in ENGS}
        self.nops = 0
        self.dma_bufs = []

    def add(self, eng, fn, reads=(), writes=(), dma=False, sembuf=None):
        op = Op()
        op.eng = eng
        op.fn = fn
        op.signal = False
        op.semval = 0
        op.dma = dma
        op.sembuf = sembuf
        op.n = self.nops
        self.nops += 1
        deps = {}

        def need(p, raw):
            if p is op:
                return
            if p.dma:
                key = ("d", id(p.sembuf))
            else:
                if p.eng == eng and not dma:
                    if not raw or eng == "pe":
                        return
                key = p.eng
            q = deps.get(key)
            if q is None or q.n < p.n:
                deps[key] = p

        for b in reads:
            for p in b.writers.values():
                need(p, True)
            if b.excl:
                for p in b.readers.values():
                    need(p, False)
        for b in writes:
            for p in b.readers.values():
                need(p, False)
        op.deps = list(deps.values())
        for p in op.deps:
            if not p.dma:
                p.signal = True
        if dma:
            if sembuf.sem is None:
                self.dma_bufs.append(sembuf)
                sembuf.sem = True
            sembuf.semcnt += 16
            op.semval = sembuf.semcnt
            key = ("d", id(sembuf))
        else:
            key = eng
        for b in reads:
            b.readers[key] = op
            if b.excl and b not in writes:
                b.writers[key] = op
        for b in writes:
            b.writers[key] = op
        self.ops[eng].append(op)
        return op

    def finalize_and_emit(self, stack):
        nc = self.nc
        engsem = {}
        for e in ("pe", "act", "dve", "pool"):
            engsem[e] = stack.enter_context(nc.semaphore("cnt_" + e))
        for i, b in enumerate(self.dma_bufs):
            b.sem = stack.enter_context(nc.semaphore("d%d" % i))
        for e in ENGS:
            c = 0
            for op in self.ops[e]:
                if op.dma:
                    continue
                if op.signal:
                    c += 1
                    op.semval = c
        block = stack.enter_context(nc.Block())

        def emit(engname):
            def run(e):
                waited = {}
                for op in self.ops[engname]:
                    for p in op.deps:
                        if p.dma:
                            sem = p.sembuf.sem
                        else:
                            sem = engsem[p.eng]
                        k = id(sem)
                        if waited.get(k, 0) < p.semval:
                            e.wait_ge(sem, p.semval)
                            waited[k] = p.semval
                    if op.fn is None:
                        continue
                    ins = op.fn(e)
                    if op.dma:
                        ins.then_inc(op.sembuf.sem, 16)
                    elif op.signal:
                        ins.then_inc(engsem[engname], 1)
            return run

        block.sync(emit("sp"))
        block.gpsimd(emit("pool"))
        block.scalar(emit("act"))
        block.vector(emit("dve"))
        block.tensor(emit("pe"))


STOP = 9
LSTAGE = 9
LSUB = 9
LCHUNKS = 12


class _Stop(Exception):
    pass


def _ckpt(k):
    if STOP == k:
        raise _Stop()


def build_nc():
    nc = bass.Bass("TRN2", target_bir_lowering=False)
    P = Prog(nc)
    try:
        _record(nc, P)
    except _Stop:
        pass
    with ExitStack() as stack:
        P.finalize_and_emit(stack)
    return nc


def _record(nc, P):

    def din(name, shape, dt=F32):
        return nc.dram_tensor(name, list(shape), dt, kind="ExternalInput").ap()

    skind = "ExternalOutput" if DEBUG else "Internal"

    def dscr(name, shape, dt=BF16):
        return nc.dram_tensor(name, list(shape), dt, kind=skind).ap()

    xs = din("xs", [S, D])
    mem = din("mem", [256, D])
    tabs_d = din("tabs", [36, 128, 128])
    params_d = din("params", [128, 240])
    lruw_d = din("lruw", [2, 2, 12, 128, 128])
    gfin_d = din("gfin", [1, D])
    ident_d = din("ident", [128, 128])
    w_in = din("w_in", [D, N_IN])
    w_gate = din("w_gate", [D, 3 * D])
    w_mem_kv = din("w_mem_kv", [D, 2048])
    w_o_attn = din("w_o_attn", [512, D])
    w_o_lru = din("w_o_lru", [1536, D])
    w_o_mem = din("w_o_mem", [1024, D])
    w_out = din("w_out", [D, D])
    w_up = din("w_up", [D, 4 * D])
    w_down = din("w_down", [4 * D, D])
    y = nc.dram_tensor("y", [OWN, D], F32, kind="ExternalOutput").ap()

    HT = dscr("HT", [4, 128, 16 * NT])
    QT = dscr("QT", [12, 128, OWN])
    KT = dscr("KT", [12, 128, 1024 + 3072])
    V = dscr("V", [1024 + 3072, 1536])
    XBT = dscr("XBT", [12, 128, 4100])
    GT = dscr("GT", [12, 128, OWN])
    YLT = dscr("YLT", [12, 128, OWN])
    YAT = dscr("YAT", [4, 128, OWN])
    bYAT = Buf("YAT")
    bHT, bQT, bKT, bV, bXBT, bGT, bYLT = (Buf(n) for n in ("HT", "QT", "KT", "V", "XBT", "GT", "YLT"))
    WUB = nc.dram_tensor("WUB", [D, 4 * D], BF16, kind="Internal").ap()
    WDB = nc.dram_tensor("WDB", [4 * D, D], BF16, kind="Internal").ap()
    bWUB, bWDB = Buf("WUB"), Buf("WDB")
    if DEBUG:
        DBG = nc.dram_tensor("DBG", [128, 4 * OWN], BF16, kind="ExternalOutput").ap()
        DBG2 = nc.dram_tensor("DBG2", [OWN, D], F32, kind="ExternalOutput").ap()
        bDBG = Buf("DBG")

    cur = [SB_BASE]

    def salloc(name, shape, dt, at=None):
        nbytes = int(np.prod(shape[1:])) * (4 if dt == F32 else 2)
        nbytes = (nbytes + 63) // 64 * 64
        if at is None:
            off = cur[0]
            cur[0] += nbytes
        else:
            off = at
        assert off + nbytes <= SB_END, (name, off, nbytes)
        t = nc.alloc_sbuf_tensor_at(name, list(shape), dt, offset=off)
        return Tens(t, Buf(name)), off, nbytes

    def S_(name, shape, dt, at=None):
        return salloc(name, shape, dt, at)[0]

    ident = S_("ident", [128, 128], BF16)
    ones = S_("ones", [128, 128], BF16)
    params = S_("params", [128, 240], F32)
    cvec = S_("cvec", [128, 24], F32)
    cvec2 = S_("cvec2", [128, 24], F32)
    small = S_("small", [128, 64], F32)
    gfin = S_("gfin", [128, D], F32)
    kcT = S_("kcT", [128, 8, 256], BF16)
    vc = S_("vc", [128, 2, 1024], BF16)
    NRING = 3
    ring_off = cur[0]
    ring = [S_("ring%d" % i, [128, 16, 512], BF16) for i in range(NRING)]
    phase_base = cur[0]

    PG_MIX, PG_MEM, PG_MLP, PG_BG, PG_TAP, PG_CB, PG_BA, PG_BI, PG_LAM = 0, 16, 32, 48, 96, 156, 168, 192, 216

    psw = [nc.alloc_psum_tensor("psw%d" % i, [128, 1024], F32) for i in range(3)]
    psf = [Tens(psw[i // 2][:, (i % 2) * 512:(i % 2 + 1) * 512], Buf("psf%d" % i, True)) for i in range(6)]
    cntw = [0]

    def next_psw():
        cntw[0] += 1
        k = cntw[0] % 3
        return psw[k], [psf[2 * k].buf, psf[2 * k + 1].buf]
    psb = [Tens(nc.alloc_psum_tensor("psb%d" % i, [128, 1024], BF16), Buf("psb%d" % i, True)) for i in range(2)]
    cnt = {"psf": 0, "psb": 0, "ring": 0, "alt": 0}

    def next_psf():
        cnt["psf"] += 1
        return psf[cnt["psf"] % 6]

    def next_psb():
        cnt["psb"] += 1
        return psb[cnt["psb"] % 2]

    def alt():
        cnt["alt"] += 1
        return cnt["alt"] % 2

    def dma(q, out, in_, reads, writes, sembuf):
        return P.add(q, lambda e: e.dma_start(out=out, in_=in_), reads, writes, dma=True, sembuf=sembuf)

    def mm(out, lhsT, rhs, start, stop, reads, writes):
        return P.add("pe", lambda e: e.matmul(out, lhsT, rhs, start=start, stop=stop), reads, writes)

    def tr(out, in_, reads, writes):
        return P.add("pe", lambda e: e.transpose(out, in_, ident.t[:, :]), list(reads) + [ident.buf], writes)

    def act(out, in_, func, reads, writes, bias=None, scale=None, accum_out=None):
        kw = {}
        if bias is not None:
            kw["bias"] = bias
        if scale is not None:
            kw["scale"] = scale
        if accum_out is not None:
            kw["accum_out"] = accum_out
        return P.add("act", lambda e: e.activation(out=out, in_=in_, func=func, **kw), reads, writes)

    def dve(fn, reads, writes):
        return P.add("dve", fn, reads, writes)

    def tt(out, in0, in1, op, reads, writes, eng="dve"):
        return P.add(eng, lambda e: e.tensor_tensor(out=out, in0=in0, in1=in1, op=op), reads, writes)

    def tcopy(out, in_, reads, writes, eng="dve"):
        return P.add(eng, lambda e: e.tensor_copy(out=out, in_=in_), reads, writes)

    def load_piece(W, r0, nkc, c0, ncol, wbuf=None):
        cnt["ring"] += 1
        slot = ring[cnt["ring"] % len(ring)]
        src = W[r0:r0 + nkc * 128, c0:c0 + ncol].rearrange("(k p) c -> p k c", p=128)
        dma("pool", slot.t[:, 0:nkc, 0:ncol], src, [] if wbuf is None else [wbuf], [slot.buf], slot.buf)
        return slot

    dma("pool", ident.t[:, :], ident_d[:, :], [], [ident.buf], ident.buf)
    dma("sp", params.t[:, :], params_d[:, :], [], [params.buf], params.buf)
    dma("sp", gfin.t[:, :], gfin_d.partition_broadcast(128), [], [gfin.buf], gfin.buf)
    dve(lambda e: e.memset(ones.t[:, :], 1.0), [], [ones.buf])
    act(small.t[:, 0:24], params.t[:, PG_LAM:PG_LAM + 24], AF.Exp, [params.buf], [small.buf], scale=-1.0)
    act(small.t[:, 24:48], small.t[:, 0:24], AF.Ln, [small.buf], [small.buf], bias=1.0)
    dve(lambda e: e.tensor_scalar(out=cvec.t[:, :], in0=small.t[:, 24:48], scalar1=-8.0, scalar2=None,
                                  op0=ALU.mult), [small.buf], [cvec.buf])
    dve(lambda e: e.tensor_scalar(out=cvec2.t[:, :], in0=small.t[:, 24:48], scalar1=-16.0, scalar2=None,
                                  op0=ALU.mult), [small.buf], [cvec2.buf])

    ssbufs = [Buf('ss%d' % i) for i in range(8)]

    def norm_p1(src_ap, src_buf, hb, ss_col):
        ssap = small.t[:, 48 + ss_col:49 + ss_col]
        rsap = small.t[:, 56 + ss_col:57 + ss_col]
        sb = ssbufs[ss_col]
        act(hb.t[:, :], src_ap, AF.Square, [src_buf], [hb.buf, sb], accum_out=ssap)
        act(ssap, ssap, AF.Sqrt, [sb], [sb], scale=1.0 / D, bias=EPS)
        dve(lambda e: e.reciprocal(out=rsap, in_=ssap), [sb], [sb])
        dve(lambda e: e.tensor_scalar(out=hb.t[:, :], in0=src_ap, scalar1=rsap, scalar2=None, op0=ALU.mult),
            [src_buf, sb], [hb.buf])

    def norm_p2(hb, dst, tok0, gcol):
        for half in range(2):
            pb = next_psb()
            for i in range(8):
                kc = half * 8 + i
                tr(pb.t[:, i * 128:(i + 1) * 128], hb.t[:, kc * 128:(kc + 1) * 128], [hb.buf], [pb.buf])
            g = params.t[:, gcol + half * 8:gcol + half * 8 + 8].unsqueeze(2).to_broadcast([128, 8, 128])
            o = dst.t[:, half * 8:half * 8 + 8, tok0:tok0 + 128]
            i0 = pb.t[:, :].rearrange("p (k t) -> p k t", k=8)
            tt(o, i0, g, ALU.mult, [pb.buf, params.buf], [dst.buf])

    def norm_block(src_ap, src_buf, hb, ss_col, dst, tok0, gcol):
        norm_p1(src_ap, src_buf, hb, ss_col)
        norm_p2(hb, dst, tok0, gcol)

    def gemm_fm(acts, W, r0, KC, c0, ncols, T, epilogue, hook=None, wbuf=None):
        for pc in range(0, ncols, 512):
            w = min(512, ncols - pc)
            slot = load_piece(W, r0, KC, c0 + pc, w, wbuf)
            for ch in range(w // 128):
                for si, (afn, abuf) in enumerate(acts):
                    pb = next_psf()
                    for kc in range(KC):
                        mm(pb.t[:, 0:T], slot.t[:, kc, ch * 128:(ch + 1) * 128], afn(kc),
                           kc == 0, kc == KC - 1, [slot.buf, abuf], [pb.buf])
                    epilogue(pc // 128 + ch, si, pb)
                    if hook is not None:
                        hook()

    cur[0] = phase_base
    zero = S_("zero", [128, 4096], BF16)
    xrow = [S_("xrow%d" % i, [128, D], F32) for i in range(2)]
    hbs = [S_("hb%d" % i, [128, D], BF16) for i in range(4)]
    hTs = [S_("hT%d" % i, [128, 16, NT], BF16) for i in range(4)]
    stg = [S_("stg%d" % i, [128, 4, NT], BF16) for i in range(3)]
    memT = S_("memT", [128, 16, 256], BF16)
    assert cur[0] <= SB_END
    cnt.update({"xrow": 0, "hb": 0, "stg": 0})

    def next_(lst, key):
        cnt[key] += 1
        return lst[cnt[key] % len(lst)]

    dve(lambda e: e.memset(zero.t[:, :], 0.0), [], [zero.buf])
    zb = Buf("zfill")
    zops = []
    zv = zero.t[:, 0:4096].rearrange("p (h c) -> p h c", h=4)
    for i in range(3):
        zops.append(dma("sp", KT[4 * i:4 * i + 4, :, 0:1024].rearrange("h p c -> p h c"), zv, [zero.buf], [bKT], zb))
    zv2 = zero.t[:, 0:4096].rearrange("p (a c) -> p a c", a=8)
    for i in range(3):
        zops.append(dma("sp", V[0:1024, i * 512:(i + 1) * 512].rearrange("(a p) c -> p a c", p=128), zv2,
                        [zero.buf], [bV], zb))
    zv3 = zero.t[:, 0:24].rearrange("p (n c) -> p n c", n=12)
    zops.append(dma("sp", XBT[:, :, 0:2].rearrange("n p c -> p n c"), zv3, [zero.buf], [bXBT], zb))
    zops.append(dma("sp", XBT[:, :, 4098:4100].rearrange("n p c -> p n c"), zv3, [zero.buf], [bXBT], zb))
    for o in zops:
        o.semval = zb.semcnt

    _ckpt(0)
    def phase_M():
        for mb in range(2):
            xr = next_(xrow, "xrow")
            dma("sp", xr.t[:, :], mem[mb * 128:(mb + 1) * 128, :], [], [xr.buf], xr.buf)
            norm_block(xr.t[:, :], xr.buf, next_(hbs, "hb"), mb, memT, mb * 128, PG_MEM)

        def ep_kc(chunk, si, pb):
            P.add("act", lambda e: e.copy(out=kcT.t[:, chunk, :], in_=pb.t[:, 0:256]), [pb.buf], [kcT.buf])

        gemm_fm([(lambda kc: memT.t[:, kc, :], memT.buf)], w_mem_kv, 0, 16, 0, 1024, 256, ep_kc)
        for pc in range(2):
            slot = load_piece(w_mem_kv, 0, 16, 1024 + pc * 512, 512)
            for mb in range(2):
                pb = next_psf()
                for kc in range(16):
                    mm(pb.t[:, :], memT.t[:, kc, mb * 128:(mb + 1) * 128], slot.t[:, kc, :], kc == 0, kc == 15,
                       [memT.buf, slot.buf], [pb.buf])
                tcopy(vc.t[:, mb, pc * 512:(pc + 1) * 512], pb.t[:, :], [pb.buf], [vc.buf])


    _ckpt(1)
    QSCALE = 1.0 / math.sqrt(128.0)
    cnt["hT"] = 0
    def prep_macro(mt):
        hts = [next_(hTs, "hT"), next_(hTs, "hT")]
        p1s, p2s = [], []
        k = 0
        for si, j in enumerate((2 * mt, 2 * mt + 1)):
            hT = hts[si]
            for b in range(4):
                st_ = {}

                def t1(j=j, b=b, st_=st_, k=k):
                    xr = next_(xrow, "xrow")
                    hb = next_(hbs, "hb")
                    st_["hb"] = hb
                    r0 = j * NT + b * 128
                    dma("sp", xr.t[:, :], xs[r0:r0 + 128, :], [], [xr.buf], xr.buf)
                    norm_p1(xr.t[:, :], xr.buf, hb, 2 + (k % 4))

                def t2(j=j, b=b, hT=hT, st_=st_):
                    norm_p2(st_["hb"], hT, b * 128, PG_MIX)
                    if b == 3 and j < 4:
                        dma("sp", HT[j], hT.t[:, :, :].rearrange("p k t -> p (k t)"), [hT.buf], [bHT], hT.buf)
                p1s.append(t1)
                p2s.append(t2)
                k += 1
        thunks = [p1s[0]]
        for i in range(1, 8):
            thunks += [p1s[i], p2s[i - 1]]
        thunks.append(p2s[7])
        return hts, thunks

    hts_next, th0 = prep_macro(0)
    for th in th0:
        th()
    for mt in range(4):
        tiles = [2 * mt, 2 * mt + 1]
        hts = hts_next
        acts = [((lambda kc, h=h: h.t[:, kc, :]), h.buf) for h in hts]

        def fm_to_dram(c0, nchunks, dst, dbuf, col0, kind, hook=None):
            state = {}

            def ep(chunk, si, pb):
                key = (chunk // 4, si)
                if key not in state:
                    state[key] = next_(stg, "stg")
                st = state[key]
                o = st.t[:, chunk % 4, :]
                if kind == "q":
                    P.add("act", lambda e: e.mul(out=o, in_=pb.t[:, :], mul=QSCALE), [pb.buf], [st.buf])
                elif kind == "gelu":
                    act(o, pb.t[:, :], AF.Gelu_apprx_tanh, [pb.buf], [st.buf])
                elif alt():
                    P.add("act", lambda e: e.copy(out=o, in_=pb.t[:, :]), [pb.buf], [st.buf])
                else:
                    tcopy(o, pb.t[:, :], [pb.buf], [st.buf])
                if chunk % 4 == 3:
                    g0 = chunk - 3
                    t0 = col0 + tiles[si] * NT
                    dma("sp", dst[g0:g0 + 4, :, t0:t0 + NT].rearrange("h p t -> p h t"), st.t[:, :, :],
                        [st.buf], [dbuf], st.buf)
            gemm_fm(acts, w_in, 0, 16, c0, nchunks * 128, NT, ep, hook)

        def v_to_dram():
            for pc in range(3):
                slot = load_piece(w_in, 0, 16, 3072 + pc * 512, 512)
                for si, h in enumerate(hts):
                    st = next_(stg, "stg")
                    for ts in range(4):
                        pb = next_psf()
                        for kc in range(16):
                            mm(pb.t[:, :], h.t[:, kc, ts * 128:(ts + 1) * 128], slot.t[:, kc, :], kc == 0, kc == 15,
                               [h.buf, slot.buf], [pb.buf])
                        if alt():
                            P.add("act", lambda e, st=st, ts=ts, pb=pb: e.copy(out=st.t[:, ts, :], in_=pb.t[:, :]),
                                  [pb.buf], [st.buf])
                        else:
                            tcopy(st.t[:, ts, :], pb.t[:, :], [pb.buf], [st.buf])
                    r0 = 1024 + tiles[si] * NT
                    dma("sp", V[r0:r0 + NT, pc * 512:(pc + 1) * 512].rearrange("(a p) c -> p a c", p=128),
                        st.t[:, :, :], [st.buf], [bV], st.buf)

        if mt < 2:
            fm_to_dram(0, 12, QT, bQT, 0, "q")
        if mt < 3:
            fm_to_dram(1536, 12, KT, bKT, 1024, "copy")
            v_to_dram()
        if mt < 2:
            fm_to_dram(6144, 12, GT, bGT, 0, "gelu")
        if mt == 0:
            phase_M()
        hook = None
        if mt < 3:
            hts_next, pend_th = prep_macro(mt + 1)
            hk = {"n": 0}

            def hook(pend_th=pend_th, hk=hk):
                hk["n"] += 1
                if hk["n"] % 3 != 0 and pend_th:
                    pend_th.pop(0)()
        fm_to_dram(4608, 12, XBT, bXBT, 2, "copy", hook)
        if mt < 3:
            while pend_th:
                pend_th.pop(0)()

    _ckpt(2)
    cur[0] = phase_base
    XBs = [S_("XB", [128, 4100], BF16)]
    XB1s = [S_("XB1", [128, 4100], BF16)]
    Gcs = [S_("Gc", [128, OWN], BF16)]
    xc32s = [S_("xc32", [128, S], F32)]
    xcbs = [S_("xcb", [128, S], BF16)]
    Ycs = [S_("Yc%d" % i, [128, OWN], BF16) for i in range(2)]
    wabs = [S_("wab%d" % i, [128, 2, 2, 128], BF16) for i in range(2)]
    diags = [S_("diag%d" % i, [128, 5, 128], BF16) for i in range(2)]
    AFb = S_("AFb", [128, OWN], F32)
    BFb = S_("BFb", [128, OWN], F32)
    identf = S_("identf", [128, 128], F32)
    Abuf = S_("Abuf", [128, S], F32)
    Bbuf = S_("Bbuf", [128, S], F32)
    tmps = [S_("ltmp%d" % i, [128, 1024], F32) for i in range(8)]
    assert cur[0] <= SB_END
    ro = ring_off
    xc32s.append(S_("xc32b", [128, S], F32, at=ro))
    xcbs.append(S_("xcbb", [128, S], BF16, at=ro + 16384))
    XBs.append(S_("XBb", [128, 4100], BF16, at=ro + 24576))
    XB1s.append(S_("XB1b", [128, 4100], BF16, at=ro + 24576 + 8256))
    Gcs.append(S_("Gcb", [128, OWN], BF16, at=ro + 24576 + 2 * 8256))
    assert 24576 + 2 * 8256 + 4096 <= NRING * 16384
    ring_alias = [xc32s[1].buf, xcbs[1].buf, XBs[1].buf, XB1s[1].buf, Gcs[1].buf]
    barrier_bufs = [zero.buf] + [t.buf for t in xrow + hbs + hTs + stg] + [memT.buf]
    lbufs = [t.buf for t in [XBs[0], XB1s[0], Gcs[0], xc32s[0], xcbs[0], AFb, BFb, identf, Abuf, Bbuf] +
             Ycs + wabs + diags + tmps]

    def alias_barrier(old_bufs, new_bufs):
        for nb in new_bufs:
            for ob in old_bufs:
                for k, p in ob.readers.items():
                    q = nb.readers.get(k)
                    if q is None or q.n < p.n:
                        nb.readers[k] = p
                for k, p in ob.writers.items():
                    q = nb.readers.get(k)
                    if q is None or q.n < p.n:
                        nb.readers[k] = p

    alias_barrier(barrier_bufs, lbufs)
    alias_barrier([r.buf for r in ring], ring_alias)
    dma("sp", identf.t[:, :], ident_d[:, :], [], [identf.buf], identf.buf)
    cnt["ltmp"] = 0

    def chunk_start(n):
        k = n % 2
        XB, XB1, Gc, wab, diag, xc32, xcb = XBs[k], XB1s[k], Gcs[k], wabs[k], diags[k], xc32s[k], xcbs[k]
        dma("sp", XB.t[:, :], XBT[n], [bXBT], [XB.buf], XB.buf)
        tcopy(XB1.t[:, 0:4098], XB.t[:, 1:4099], [XB.buf], [XB1.buf])
        dma("sp", Gc.t[:, :], GT[n], [bGT], [Gc.buf], Gc.buf)
        dma("pool", wab.t[:, :, :, :], lruw_d[:, :, n].rearrange("r k c d -> c r k d"), [], [wab.buf], wab.buf)
        for o in range(5):
            tap = params.t[:, PG_TAP + n * 5 + o:PG_TAP + n * 5 + o + 1]
            dve(lambda e, o=o, tap=tap: e.tensor_scalar(out=diag.t[:, o, :], in0=identf.t[:, :], scalar1=tap,
                                                         scalar2=None, op0=ALU.mult),
                [identf.buf, params.buf], [diag.buf])
        cb = params.t[:, PG_CB + n:PG_CB + n + 1]
        for cg in range(4):
            w_, wb_ = next_psw()
            for half in range(2):
                st = cg * 2 + half
                for o in range(5):
                    src = XB if o % 2 == 0 else XB1
                    c0 = st * 512 + (o if o % 2 == 0 else o - 1)
                    mm(w_[:, half * 512:(half + 1) * 512], diag.t[:, o, :], src.t[:, c0:c0 + 512], o == 0, o == 4,
                       [diag.buf, src.buf], [wb_[half]])
            sl = slice(cg * 1024, (cg + 1) * 1024)
            dve(lambda e, sl=sl, w_=w_, cb=cb: e.tensor_scalar(out=xc32.t[:, sl], in0=w_[:, :], scalar1=cb, scalar2=None,
                                                               op0=ALU.add), wb_ + [params.buf], [xc32.buf])
            act(xcb.t[:, sl], w_[:, :], AF.Identity, wb_ + [params.buf], [xcb.buf], bias=cb)

    def lru_dir(n, role, ngr, Adst, Bdst, start_col, abuf, bbuf):
        k = n % 2
        wab, xc32, xcb = wabs[k], xc32s[k], xcbs[k]
        ba = params.t[:, PG_BA + role * 12 + n:PG_BA + role * 12 + n + 1]
        bi = params.t[:, PG_BI + role * 12 + n:PG_BI + role * 12 + n + 1]
        cv = cvec.t[:, role * 12 + n:role * 12 + n + 1]
        pend = []

        def stage2(gi, sl, te2, tb):
            act(te2.t[:, :], te2.t[:, :], AF.Sqrt, [te2.buf], [te2.buf], scale=-1.0, bias=1.0)
            tt(Bdst(sl), tb.t[:, :], te2.t[:, :], ALU.mult, [tb.buf, te2.buf], [bbuf])
            if start_col // 1024 == gi:
                c = start_col % 1024
                tcopy(Bdst(slice(start_col, start_col + 1)), tb.t[:, c:c + 1], [tb.buf], [bbuf])

        for gi in range(ngr):
            sl = slice(gi * 1024, (gi + 1) * 1024)
            wr, wrb = next_psw()
            for half in range(2):
                hs = slice(gi * 1024 + half * 512, gi * 1024 + (half + 1) * 512)
                mm(wr[:, half * 512:(half + 1) * 512], wab.t[:, role, 0, :], xcb.t[:, hs], True, True,
                   [wab.buf, xcb.buf], [wrb[half]])
            wi, wib = next_psw()
            for half in range(2):
                hs = slice(gi * 1024 + half * 512, gi * 1024 + (half + 1) * 512)
                mm(wi[:, half * 512:(half + 1) * 512], wab.t[:, role, 1, :], xcb.t[:, hs], True, True,
                   [wab.buf, xcb.buf], [wib[half]])
            tr_ = next_(tmps, "ltmp")
            ti = next_(tmps, "ltmp")
            te2 = next_(tmps, "ltmp")
            tb = next_(tmps, "ltmp")
            act(tr_.t[:, :], wr[:, :], AF.Sigmoid, wrb + [params.buf], [tr_.buf], bias=ba)
            act(ti.t[:, :], wi[:, :], AF.Sigmoid, wib + [params.buf], [ti.buf], bias=bi)
            a_ap = Adst(sl)
            act(a_ap, tr_.t[:, :], AF.Exp, [tr_.buf, cvec.buf], [abuf], scale=cv)
            tt(te2.t[:, :], a_ap, a_ap, ALU.mult, [abuf], [te2.buf])
            tt(tb.t[:, :], ti.t[:, :], xc32.t[:, sl], ALU.mult, [ti.buf, xc32.buf], [tb.buf])
            if pend:
                stage2(*pend.pop())
            pend.append((gi, sl, te2, tb))
        stage2(*pend.pop())

    conv_th = []
    for i in range(16):
        conv_th.append(lambda i=i: dma("pool", WUB[i * 128:(i + 1) * 128, :], w_up[i * 128:(i + 1) * 128, :],
                                       [], [bWUB], bWUB))
    for i in range(16):
        conv_th.append(lambda i=i: dma("pool", WDB[i * 512:(i + 1) * 512, :].rearrange("(a p) c -> p a c", p=128),
                                       w_down[i * 512:(i + 1) * 512, :].rearrange("(a p) c -> p a c", p=128),
                                       [], [bWDB], bWDB))
    if LCHUNKS > 0:
        chunk_start(0)
    for n in range(LCHUNKS):
        k = n % 2
        xc32, Gc, Yc = xc32s[k], Gcs[k], Ycs[k]
        lru_dir(n, 0, 2, lambda sl: AFb.t[:, sl], lambda sl: BFb.t[:, sl], 0, AFb.buf, BFb.buf)
        dve(lambda e: e.tensor_tensor_scan(out=BFb.t[:, :], data0=AFb.t[:, :], data1=BFb.t[:, :],
                                           initial=0.0, op0=ALU.mult, op1=ALU.add), [AFb.buf, BFb.buf], [BFb.buf])
        lru_dir(n, 1, 4, lambda sl: Abuf.t[:, sl], lambda sl: Bbuf.t[:, sl], 4095, Abuf.buf, Bbuf.buf)
        if n + 1 < LCHUNKS:
            chunk_start(n + 1)
        for _ in range(3):
            if conv_th:
                conv_th.pop(0)()
        dve(lambda e, xc32=xc32: e.tensor_tensor_scan(out=xc32.t[:, ::-1], data0=Abuf.t[:, ::-1],
                                                      data1=Bbuf.t[:, ::-1], initial=0.0, op0=ALU.mult,
                                                      op1=ALU.add),
            [Abuf.buf, Bbuf.buf], [xc32.buf])
        tt(BFb.t[:, :], xc32.t[:, 0:OWN], BFb.t[:, :], ALU.add, [xc32.buf, BFb.buf], [BFb.buf])
        tt(Yc.t[:, :], BFb.t[:, :], Gc.t[:, :], ALU.mult, [BFb.buf, Gc.buf], [Yc.buf])
        dma("sp", YLT[n], Yc.t[:, :], [Yc.buf], [bYLT], Yc.buf)
    while conv_th:
        conv_th.pop(0)()
    alias_barrier(ring_alias, [r.buf for r in ring])
    lbufs = lbufs + ring_alias

    _ckpt(3)
    cur[0] = phase_base
    a_base = cur[0]
    YA = S_("YA", [128, OWN], BF16)
    tabs = S_("tabs", [128, 36, 128], BF16)
    Qh = [S_("Qh%d" % i, [128, OWN], BF16) for i in range(2)]
    Kh = [S_("Kh%d" % i, [128, 4096], BF16) for i in range(2)]
    Vh = [S_("Vh%d" % i, [128, 32, 128], BF16) for i in range(2)]
    acc = S_("acc", [128, 2, OWN], F32)
    Pt = [S_("Pt%d" % i, [128, 2, 128], BF16) for i in range(5)]
    assert cur[0] <= SB_END
    abufs = [YA.buf, acc.buf, tabs.buf] + [t.buf for t in Qh + Kh + Vh + Pt]
    acc_alt_holder = []
    alias_barrier(lbufs, abufs)
    cnt.update({"Qh": 0, "Kh": 0, "Vh": 0, "Pt": 0})
    dma("pool", tabs.t[:, :, :], tabs_d.rearrange("t k q -> k t q"), [], [tabs.buf], tabs.buf)

    accb = [acc.buf, Buf("acc_alt")]
    alias_barrier(lbufs, [accb[1]])
    for hh in range(4):
        for g, d in enumerate((1, 4, 16)):
            head = 4 * g + hh
            nb = 16 // d
            nkt = nb + 1
            q_t = next_(Qh, "Qh")
            k_t = next_(Kh, "Kh")
            v_t = next_(Vh, "Vh")
            dma("sp", q_t.t[:, :], QT[head], [bQT], [q_t.buf], q_t.buf)
            c_lo = 1024 - 64 * d
            c_hi = 1024 + OWN + 64 * d
            dma("sp", k_t.t[:, c_lo:c_hi], KT[head, :, c_lo:c_hi], [bKT], [k_t.buf], k_t.buf)
            nrow = d * 128 * nkt
            for r in range(d):
                src = V[c_lo + r:c_lo + nrow:d, head * 128:(head + 1) * 128].rearrange("(m p) c -> p m c", p=128)
                dma("sp", v_t.t[:, r * nkt:(r + 1) * nkt, :], src, [bV], [v_t.buf], v_t.buf)
            def scores(r, n):
                q0 = r + d * 128 * n
                qs = slice(q0, q0 + d * 127 + 1, d)
                ps = next_psf()
                for mi in range(2):
                    m = n + mi
                    k0 = c_lo + r + 128 * d * m
                    ks = slice(k0, k0 + 127 * d + 1, d)
                    variant = 2 if mi == 1 else (0 if n == 0 else 1)
                    mm(ps.t[:, mi * 128:(mi + 1) * 128], k_t.t[:, ks], q_t.t[:, qs], True, False,
                       [k_t.buf, q_t.buf], [ps.buf])
                    mm(ps.t[:, mi * 128:(mi + 1) * 128], ident.t[:, :], tabs.t[:, head * 3 + variant, :], False,
                       True, [ident.buf, tabs.buf], [ps.buf])
                pt = next_(Pt, "Pt")
                act(pt.t[:, :, :], ps.t[:, 0:256].rearrange("p (a q) -> p a q", a=2), AF.Exp, [ps.buf], [pt.buf])
                return (r, n, qs, pt)

            def pv(r, n, qs, pt):
                p2 = next_psf()
                for mi in range(2):
                    m = n + mi
                    mm(p2.t[:, 0:128], v_t.t[:, r * nkt + m, :], pt.t[:, mi, :], mi == 0, mi == 1,
                       [v_t.buf, pt.buf], [p2.buf])
                for mi in range(2):
                    mm(p2.t[:, 128:256], ones.t[:, :], pt.t[:, mi, :], mi == 0, mi == 1,
                       [ones.buf, pt.buf], [p2.buf])
                o = acc.t[:, :, qs]
                i0_ = p2.t[:, 0:256].rearrange("p (a q) -> p a q", a=2)
                if g == 0:
                    tcopy(o, i0_, [p2.buf], [accb[0]])
                else:
                    tt(o, i0_, o, ALU.add, [p2.buf, accb[(g - 1) % 2]], [accb[g % 2]])

            pend = []
            for r in range(d):
                for n in range(nb):
                    pend.append(scores(r, n))
                    if len(pend) > 2:
                        pv(*pend.pop(0))
            while pend:
                pv(*pend.pop(0))
        dve(lambda e: e.reciprocal(out=acc.t[:, 1, :], in_=acc.t[:, 1, :]), accb, accb)
        tt(YA.t[:, :], acc.t[:, 0, :], acc.t[:, 1, :], ALU.mult, accb, [YA.buf])
        dma("sp", YAT[hh], YA.t[:, :], [YA.buf], [bYAT], YA.buf)


    _ckpt(4)
    cur[0] = a_base
    xm = S_("xm", [128, 4, D], F32)
    hT16 = S_("hT16", [128, 16, NT], BF16)
    R32 = S_("R32", [128, 16384], BF16)
    mixT = S_("mixT", [128, 16, NT], BF16)
    mixf = S_("mixf", [128, 4, NT], F32)
    gt_ = [S_("gt%d" % i, [128, NT], F32) for i in range(2)]
    tf_ = [S_("tf%d" % i, [128, NT], F32) for i in range(2)]
    hb2 = [S_("hbm%d" % i, [128, D], BF16) for i in range(1)]
    yat = S_("yat", [128, 4, NT], BF16)
    ring.append(S_("ring3", [128, 16, 512], BF16))
    assert cur[0] <= SB_END, cur[0]
    b_yl, b_yc, b_qc, b_Pc = Buf('r_yl'), Buf('r_yc'), Buf('r_qc'), Buf('r_Pc')
    RBA = [b_yl, b_yc, b_qc, b_Pc]
    gbufs = [xm.buf, hT16.buf, b_yl, b_yc, b_qc, b_Pc, mixT.buf, mixf.buf, yat.buf, ring[3].buf] + [t.buf for t in gt_ + tf_ + hb2]
    alias_barrier(abufs, gbufs)
    cnt.update({"gt": 0, "tf": 0, "hbm": 0})
    yl = R32.t[:, 0:6144].rearrange("p (n t) -> p n t", n=12)
    yc = R32.t[:, 6144:10240].rearrange("p (n t) -> p n t", n=8)
    qc = R32.t[:, 10240:14336].rearrange("p (n t) -> p n t", n=8)
    Pc = R32.t[:, 14336:16384].rearrange("p (n t) -> p n t", n=4)
    uT = R32.t[:, :].rearrange("p (n t) -> p n t", n=32)

    for j in range(4):
        t0 = j * NT
        if j == 0:
            dma("sp", hT16.t[:, :, :].rearrange("p k t -> p (k t)"), HT[j], [bHT], [hT16.buf], hT16.buf)
        dma("sp", yat.t[:, :, :], YAT[:, :, t0:t0 + NT].rearrange("h p t -> p h t"), [bYAT], [yat.buf], yat.buf)
        dma("sp", yl, YLT[:, :, t0:t0 + NT].rearrange("n p t -> p n t"), [bYLT], [b_yl], b_yl)
        dma("sp", xm.t[:, :, :], xs[t0:t0 + NT, :].rearrange("(a p) c -> p a c", p=128), [], [xm.buf], xm.buf)

        def ep_qc(chunk, si, pb):
            P.add("act", lambda e: e.mul(out=qc[:, chunk, :], in_=pb.t[:, :], mul=1.0 / 16.0), [pb.buf], [b_qc])
        gemm_fm([(lambda kc: hT16.t[:, kc, :], hT16.buf)], w_in, 0, 16, 7680, 1024, NT, ep_qc)

        def xa_s(h):
            for mc in range(2):
                pb = next_psf()
                for cc in range(2):
                    mm(pb.t[:, :], kcT.t[:, h * 2 + cc, mc * 128:(mc + 1) * 128], qc[:, h * 2 + cc, :], cc == 0, cc == 1,
                       [kcT.buf, b_qc], [pb.buf])
                act(Pc[:, (h % 2) * 2 + mc, :], pb.t[:, :], AF.Exp, [pb.buf], [b_Pc])

        def xa_p(h):
            pd = next_psf()
            for mc in range(2):
                mm(pd.t[:, :], ones.t[:, :], Pc[:, (h % 2) * 2 + mc, :], mc == 0, mc == 1, [ones.buf, b_Pc], [pd.buf])
            rec = next_(tf_, "tf")
            dve(lambda e, rec=rec, pd=pd: e.reciprocal(out=rec.t[:, :], in_=pd.t[:, :]), [pd.buf], [rec.buf])
            for cc in range(2):
                pb = next_psf()
                for mc in range(2):
                    mm(pb.t[:, :], vc.t[:, mc, h * 256 + cc * 128:h * 256 + (cc + 1) * 128], Pc[:, (h % 2) * 2 + mc, :],
                       mc == 0, mc == 1, [vc.buf, b_Pc], [pb.buf])
                tt(yc[:, h * 2 + cc, :], pb.t[:, :], rec.t[:, :], ALU.mult, [pb.buf, rec.buf], [b_yc])

        xa_thunks = [lambda: xa_s(0), lambda: xa_s(1), lambda: xa_p(0), lambda: xa_s(2), lambda: xa_p(1),
                     lambda: xa_s(3), lambda: xa_p(2), lambda: xa_p(3)]

        branches = [
            (w_o_attn, 4, lambda kc: yat.t[:, kc, :], yat.buf),
            (w_o_lru, 12, lambda kc: yl[:, kc, :], b_yl),
            (w_o_mem, 8, lambda kc: yc[:, kc, :], b_yc),
        ]
        for fb in range(4):
            for br, (Wb, KCb, afn, abuf) in enumerate(branches):
                gslot = load_piece(w_gate, 0, 16, br * D + fb * 512, 512)
                oslot = load_piece(Wb, 0, KCb, fb * 512, 512)
                if br == 2:
                    while xa_thunks:
                        xa_thunks.pop(0)()
                for ch in range(4):
                    fcg = fb * 4 + ch
                    if xa_thunks:
                        xa_thunks.pop(0)()
                    pg = next_psf()
                    for kc in range(16):
                        mm(pg.t[:, :], gslot.t[:, kc, ch * 128:(ch + 1) * 128], hT16.t[:, kc, :], kc == 0, kc == 15,
                           [gslot.buf, hT16.buf], [pg.buf])
                    g_ = next_(gt_, "gt")
                    bg = params.t[:, PG_BG + br * 16 + fcg:PG_BG + br * 16 + fcg + 1]
                    act(g_.t[:, :], pg.t[:, :], AF.Sigmoid, [pg.buf, params.buf], [g_.buf], bias=bg)
                    po = next_psf()
                    for kc in range(KCb):
                        mm(po.t[:, :], oslot.t[:, kc, ch * 128:(ch + 1) * 128], afn(kc), kc == 0, kc == KCb - 1,
                           [oslot.buf, abuf], [po.buf])
                    if br == 0:
                        tt(mixf.t[:, ch, :], po.t[:, :], g_.t[:, :], ALU.mult, [po.buf, g_.buf], [mixf.buf])
                    else:
                        tf = next_(tf_, "tf")
                        tt(tf.t[:, :], po.t[:, :], g_.t[:, :], ALU.mult, [po.buf, g_.buf], [tf.buf])
                        if br == 1:
                            tt(mixf.t[:, ch, :], mixf.t[:, ch, :], tf.t[:, :], ALU.add, [mixf.buf, tf.buf], [mixf.buf])
                        else:
                            tt(mixT.t[:, fcg, :], mixf.t[:, ch, :], tf.t[:, :], ALU.add, [mixf.buf, tf.buf],
                               [mixT.buf])

        for cb_ in range(4):
            slot = load_piece(w_out, 0, 16, cb_ * 512, 512)
            cs = slice(cb_ * 512, (cb_ + 1) * 512)
            for ts in range(4):
                pb = next_psf()
                for kc in range(16):
                    mm(pb.t[:, :], mixT.t[:, kc, ts * 128:(ts + 1) * 128], slot.t[:, kc, :], kc == 0, kc == 15,
                       [mixT.buf, slot.buf], [pb.buf])
                tt(xm.t[:, ts, cs], pb.t[:, :], xm.t[:, ts, cs], ALU.add, [pb.buf, xm.buf], [xm.buf])
        if DEBUG:
            dma("sp", DBG2[t0:t0 + NT, :].rearrange("(a p) c -> p a c", p=128), xm.t[:, :, :], [xm.buf], [bDBG],
                xm.buf)

        for ts in range(4):
            norm_block(xm.t[:, ts, :], xm.buf, next_(hb2, "hbm"), 2 + ts, hT16, ts * 128, PG_MLP)
        for fh in range(2):
            def ep_up(chunk, si, pb):
                tf = next_(tf_, "tf")
                act(tf.t[:, :], pb.t[:, :], AF.Relu, [pb.buf], [tf.buf])
                tt(uT[:, chunk, :], tf.t[:, :], tf.t[:, :], ALU.mult, [tf.buf], RBA)
            gemm_fm([(lambda kc: hT16.t[:, kc, :], hT16.buf)], WUB, 0, 16, fh * 4096, 4096, NT, ep_up, wbuf=bWUB)
            if fh == 1 and j < 3:
                dma("sp", hT16.t[:, :, :].rearrange("p k t -> p (k t)"), HT[j + 1], [bHT], [hT16.buf], hT16.buf)
            for cb_ in range(4):
                cs = slice(cb_ * 512, (cb_ + 1) * 512)
                pbs = [next_psf() for _ in range(4)]
                for kp in range(2):
                    slot = load_piece(WDB, (fh * 32 + kp * 16) * 128, 16, cb_ * 512, 512, bWDB)
                    for ts in range(4):
                        for kc in range(16):
                            mm(pbs[ts].t[:, :], uT[:, kp * 16 + kc, ts * 128:(ts + 1) * 128], slot.t[:, kc, :],
                               kp == 0 and kc == 0, kp == 1 and kc == 15, RBA + [slot.buf], [pbs[ts].buf])
                for ts in range(4):
                    tt(xm.t[:, ts, cs], pbs[ts].t[:, :], xm.t[:, ts, cs], ALU.add, [pbs[ts].buf, xm.buf], [xm.buf])

        for ts in range(4):
            hb = next_(hb2, "hbm")
            ssap = small.t[:, 50 + ts:51 + ts]
            rsap = small.t[:, 58 + ts:59 + ts]
            sb = ssbufs[2 + ts]
            act(hb.t[:, :], xm.t[:, ts, :], AF.Square, [xm.buf], [hb.buf, sb], accum_out=ssap)
            act(ssap, ssap, AF.Sqrt, [sb], [sb], scale=1.0 / D, bias=EPS)
            dve(lambda e, rsap=rsap, ssap=ssap: e.reciprocal(out=rsap, in_=ssap), [sb], [sb])
            dve(lambda e, ts=ts, rsap=rsap: e.scalar_tensor_tensor(out=xm.t[:, ts, :], in0=xm.t[:, ts, :], scalar=rsap,
                                                                   in1=gfin.t[:, :], op0=ALU.mult, op1=ALU.mult),
                [xm.buf, sb, gfin.buf], [xm.buf])
        st_op = dma("sp", y[t0:t0 + NT, :].rearrange("(a p) c -> p a c", p=128), xm.t[:, :, :], [xm.buf], [], xm.buf)

    P.add("sp", None, [xm.buf], [xm.buf])


def _t5_bucket(rel):
    nb = 16
    max_exact = 8
    sign = (rel > 0).astype(np.int32) * nb
    n = np.abs(rel)
    large = max_exact + (np.log(np.maximum(n, 1) / max_exact) / np.log(1024 / max_exact) * (nb - max_exact)).astype(np.int32)
    large = np.minimum(large, nb - 1)
    return (sign + np.where(n < max_exact, n, large)).astype(np.int32)


def _tables(rel_bias, par):
    tabs = np.empty((36, 128, 128), np.float32)
    k = np.arange(128)[:, None]
    q = np.arange(128)[None, :]
    for g, d in enumerate((1, 4, 16)):
        for hh in range(4):
            head = 4 * g + hh
            for variant in range(3):
                delta = 64 if variant == 2 else -64
                rel = delta + k - q
                band = np.abs(rel) <= 64
                if variant == 0:
                    band = band & (k >= 64)
                relt = rel * d
                if par:
                    relt = -relt
                vals = rel_bias[_t5_bucket(relt), head]
                tabs[head * 3 + variant] = np.where(band, vals, np.float32(-30000.0))
    return tabs


def _col(v, n):
    return np.ascontiguousarray(v.reshape(n, 128).T)


def _prep(inputs):
    f = lambda k: np.asarray(inputs[k], dtype=np.float32)
    x, mem, rel_bias = f("x"), f("mem"), f("rel_bias")
    conv_w, conv_b = f("conv_w")[0], f("conv_b")[0]
    lru_wa, lru_wi = f("lru_wa")[0], f("lru_wi")[0]
    lru_ba, lru_bi, lru_lam = f("lru_ba")[0], f("lru_bi")[0], f("lru_lambda")[0]
    shared = {
        "gfin": f("norm_final").reshape(1, D),
        "ident": np.eye(128, dtype=np.float32),
        "w_in": f("w_in")[0], "w_gate": f("w_gate")[0], "w_mem_kv": f("w_mem_kv")[0],
        "w_o_attn": f("w_o_attn")[0], "w_o_lru": f("w_o_lru")[0], "w_o_mem": f("w_o_mem")[0],
        "w_out": f("w_out")[0], "w_up": f("w_up")[0], "w_down": f("w_down")[0],
    }
    per_par = []
    for par in range(2):
        roles = (0, 1) if par == 0 else (1, 0)
        params = np.zeros((128, 240), np.float32)
        params[:, 0:16] = _col(f("norm_mix")[0], 16)
        params[:, 16:32] = _col(f("norm_mem")[0], 16)
        params[:, 32:48] = _col(f("norm_mlp")[0], 16)
        params[:, 48:96] = _col(f("b_gate")[0], 48)
        taps = np.zeros((5, 1536), np.float32)
        if par == 0:
            taps[1:5] = conv_w
        else:
            taps[0:4] = conv_w[::-1]
        params[:, 96:156] = taps.reshape(5, 12, 128).transpose(2, 1, 0).reshape(128, 60)
        params[:, 156:168] = _col(conv_b, 12)
        for ri, dr in enumerate(roles):
            params[:, 168 + ri * 12:168 + ri * 12 + 12] = lru_ba[dr].T
            params[:, 192 + ri * 12:192 + ri * 12 + 12] = lru_bi[dr].T
            params[:, 216 + ri * 12:216 + ri * 12 + 12] = _col(lru_lam[dr], 12)
        lruw = np.stack([np.stack([lru_wa[dr], lru_wi[dr]]) for dr in roles])
        per_par.append({"params": params, "lruw": np.ascontiguousarray(lruw), "tabs": _tables(rel_bias, par)})

    in_maps = []
    for c in range(8):
        b, par = c // 2, c % 2
        m = dict(shared)
        m.update(per_par[par])
        m["xs"] = np.ascontiguousarray(x[b] if par == 0 else x[b, ::-1])
        m["mem"] = np.ascontiguousarray(mem[b])
        in_maps.append(m)
    return in_maps


def kernel(**inputs):
    in_maps = _prep(inputs)
    nc = build_nc()
    res = run_bass_kernel_spmd(nc, in_maps, core_ids=list(range(8)))
    out = np.empty((4, S, D), np.float32)
    for c in range(8):
        b, par = c // 2, c % 2
        yv = np.asarray(res.results[c]["y"], dtype=np.float32)
        if par == 0:
            out[b, 0:OWN] = yv
        else:
            out[b, OWN:S] = yv[::-1]
    if DEBUG:
        kernel.last = res
    return out
```

```python
import math
from contextlib import ExitStack

import numpy as np
import concourse.bass as bass
import concourse.mybir as mybir
from concourse.bass_utils import run_bass_kernel_spmd

F32 = mybir.dt.float32
BF16 = mybir.dt.bfloat16
AF = mybir.ActivationFunctionType
ALU = mybir.AluOpType

D = 2048
S = 4096
OWN = 2048
NT = 512
EPS = 1e-6
N_IN = 8704
DEBUG = False

SB_BASE = 16512
SB_END = 229344


class Buf:
    __slots__ = ("name", "writers", "readers", "sem", "semcnt", "excl")

    def __init__(self, name, excl=False):
        self.name = name
        self.excl = excl
        self.writers = {}
        self.readers = {}
        self.sem = None
        self.semcnt = 0


class Op:
    __slots__ = ("eng", "fn", "deps", "signal", "semval", "dma", "sembuf", "n")


class Tens:
    def __init__(self, t, buf):
        self.t = t
        self.buf = buf


ENGS = ("pe", "act", "dve", "pool", "sp")


class Prog:
    def __init__(self, nc):
        self.nc = nc
        self.ops = {e: [] for e in ENGS}
        self.nops = 0
        self.dma_bufs = []

    def add(self, eng, fn, reads=(), writes=(), dma=False, sembuf=None):
        op = Op()
        op.eng = eng
        op.fn = fn
        op.signal = False
        op.semval = 0
        op.dma = dma
        op.sembuf = sembuf
        op.n = self.nops
        self.nops += 1
        deps = {}

        def need(p, raw):
            if p is op:
                return
            if p.dma:
                key = ("d", id(p.sembuf))
            else:
                if p.eng == eng and not dma:
                    if not raw or eng == "pe":
                        return
                key = p.eng
            q = deps.get(key)
            if q is None or q.n < p.n:
                deps[key] = p

        for b in reads:
            for p in b.writers.values():
                need(p, True)
            if b.excl:
                for p in b.readers.values():
                    need(p, False)
        for b in writes:
            for p in b.readers.values():
                need(p, False)
        op.deps = list(deps.values())
        for p in op.deps:
            if not p.dma:
                p.signal = True
        if dma:
            if sembuf.sem is None:
                self.dma_bufs.append(sembuf)
                sembuf.sem = True
            sembuf.semcnt += 16
            op.semval = sembuf.semcnt
            key = ("d", id(sembuf))
        else:
            key = eng
        for b in reads:
            b.readers[key] = op
            if b.excl and b not in writes:
                b.writers[key] = op
        for b in writes:
            b.writers[key] = op
        self.ops[eng].append(op)
        return op

    def finalize_and_emit(self, stack):
        nc = self.nc
        engsem = {}
        for e in ("pe", "act", "dve", "pool"):
            engsem[e] = stack.enter_context(nc.semaphore("cnt_" + e))
        for i, b in enumerate(self.dma_bufs):
            b.sem = stack.enter_context(nc.semaphore("d%d" % i))
        for e in ENGS:
            c = 0
            for op in self.ops[e]:
                if op.dma:
                    continue
                if op.signal:
                    c += 1
                    op.semval = c
        block = stack.enter_context(nc.Block())

        def emit(engname):
            def run(e):
                waited = {}
                for op in self.ops[engname]:
                    for p in op.deps:
                        if p.dma:
                            sem = p.sembuf.sem
                        else:
                            sem = engsem[p.eng]
                        k = id(sem)
                        if waited.get(k, 0) < p.semval:
                            e.wait_ge(sem, p.semval)
                            waited[k] = p.semval
                    if op.fn is None:
                        continue
                    ins = op.fn(e)
                    if op.dma:
                        ins.then_inc(op.sembuf.sem, 16)
                    elif op.signal:
                        ins.then_inc(engsem[engname], 1)
            return run

        block.sync(emit("sp"))
        block.gpsimd(emit("pool"))
        block.scalar(emit("act"))
        block.vector(emit("dve"))
        block.tensor(emit("pe"))


STOP = 9
LSTAGE = 9
LSUB = 9
LCHUNKS = 12


class _Stop(Exception):
    pass


def _ckpt(k):
    if STOP == k:
        raise _Stop()


def build_nc():
    nc = bass.Bass("TRN2", target_bir_lowering=False)
    P = Prog(nc)
    try:
        _record(nc, P)
    except _Stop:
        pass
    with ExitStack() as stack:
        P.finalize_and_emit(stack)
    return nc


def _record(nc, P):

    def din(name, shape, dt=F32):
        return nc.dram_tensor(name, list(shape), dt, kind="ExternalInput").ap()

    skind = "ExternalOutput" if DEBUG else "Internal"

    def dscr(name, shape, dt=BF16):
        return nc.dram_tensor(name, list(shape), dt, kind=skind).ap()

    xs = din("xs", [S, D])
    mem = din("mem", [256, D])
    tabs_d = din("tabs", [36, 128, 128])
    params_d = din("params", [128, 240])
    lruw_d = din("lruw", [2, 2, 12, 128, 128])
    gfin_d = din("gfin", [1, D])
    ident_d = din("ident", [128, 128])
    w_in = din("w_in", [D, N_IN])
    w_gate = din("w_gate", [D, 3 * D])
    w_mem_kv = din("w_mem_kv", [D, 2048])
    w_o_attn = din("w_o_attn", [512, D])
    w_o_lru = din("w_o_lru", [1536, D])
    w_o_mem = din("w_o_mem", [1024, D])
    w_out = din("w_out", [D, D])
    w_up = din("w_up", [D, 4 * D])
    w_down = din("w_down", [4 * D, D])
    y = nc.dram_tensor("y", [OWN, D], F32, kind="ExternalOutput").ap()

    HT = dscr("HT", [4, 128, 16 * NT])
    QT = dscr("QT", [12, 128, OWN])
    KT = dscr("KT", [12, 128, 1024 + 3072])
    V = dscr("V", [1024 + 3072, 1536])
    XBT = dscr("XBT", [12, 128, 4160])
    GT = dscr("GT", [12, 128, OWN])
    YLT = dscr("YLT", [12, 128, OWN])
    YAT = dscr("YAT", [4, 128, OWN])
    bYAT = Buf("YAT")
    bHT, bQT, bKT, bV, bXBT, bGT, bYLT = (Buf(n) for n in ("HT", "QT", "KT", "V", "XBT", "GT", "YLT"))
    WUB = nc.dram_tensor("WUB", [D, 4 * D], BF16, kind="Internal").ap()
    WDB = nc.dram_tensor("WDB", [4 * D, D], BF16, kind="Internal").ap()
    bWUB, bWDB = Buf("WUB"), Buf("WDB")
    if DEBUG:
        DBG = nc.dram_tensor("DBG", [128, 4 * OWN], BF16, kind="ExternalOutput").ap()
        DBG2 = nc.dram_tensor("DBG2", [OWN, D], F32, kind="ExternalOutput").ap()
        bDBG = Buf("DBG")

    cur = [SB_BASE]

    def salloc(name, shape, dt, at=None):
        nbytes = int(np.prod(shape[1:])) * (4 if dt == F32 else 2)
        nbytes = (nbytes + 63) // 64 * 64
        if at is None:
            off = cur[0]
            cur[0] += nbytes
        else:
            off = at
        assert off + nbytes <= SB_END, (name, off, nbytes)
        t = nc.alloc_sbuf_tensor_at(name, list(shape), dt, offset=off)
        return Tens(t, Buf(name)), off, nbytes

    def S_(name, shape, dt, at=None):
        return salloc(name, shape, dt, at)[0]

    ident = S_("ident", [128, 128], BF16)
    ones = S_("ones", [128, 128], BF16)
    params = S_("params", [128, 240], F32)
    cvec = S_("cvec", [128, 24], F32)
    cvec2 = S_("cvec2", [128, 24], F32)
    small = S_("small", [128, 64], F32)
    gfin = S_("gfin", [128, D], F32)
    kcT = S_("kcT", [128, 8, 256], BF16)
    vc = S_("vc", [128, 2, 1024], BF16)
    NRING = 3
    ring_off = cur[0]
    ring = [S_("ring%d" % i, [128, 16, 512], BF16) for i in range(NRING)]
    phase_base = cur[0]

    PG_MIX, PG_MEM, PG_MLP, PG_BG, PG_TAP, PG_CB, PG_BA, PG_BI, PG_LAM = 0, 16, 32, 48, 96, 156, 168, 192, 216

    psw = [nc.alloc_psum_tensor("psw%d" % i, [128, 1024], F32) for i in range(3)]
    psf = [Tens(psw[i // 2][:, (i % 2) * 512:(i % 2 + 1) * 512], Buf("psf%d" % i, True)) for i in range(6)]
    cntw = [0]

    def next_psw():
        cntw[0] += 1
        k = cntw[0] % 3
        return psw[k], [psf[2 * k].buf, psf[2 * k + 1].buf]
    psb = [Tens(nc.alloc_psum_tensor("psb%d" % i, [128, 1024], BF16), Buf("psb%d" % i, True)) for i in range(2)]
    cnt = {"psf": 0, "psb": 0, "ring": 0, "alt": 0}

    def next_psf():
        cnt["psf"] += 1
        return psf[cnt["psf"] % 6]

    def next_psb():
        cnt["psb"] += 1
        return psb[cnt["psb"] % 2]

    def alt():
        cnt["alt"] += 1
        return cnt["alt"] % 2

    def dma(q, out, in_, reads, writes, sembuf):
        return P.add(q, lambda e: e.dma_start(out=out, in_=in_), reads, writes, dma=True, sembuf=sembuf)

    def mm(out, lhsT, rhs, start, stop, reads, writes):
        return P.add("pe", lambda e: e.matmul(out, lhsT, rhs, start=start, stop=stop), reads, writes)

    def tr(out, in_, reads, writes):
        return P.add("pe", lambda e: e.transpose(out, in_, ident.t[:, :]), list(reads) + [ident.buf], writes)

    def act(out, in_, func, reads, writes, bias=None, scale=None, accum_out=None):
        kw = {}
        if bias is not None:
            kw["bias"] = bias
        if scale is not None:
            kw["scale"] = scale
        if accum_out is not None:
            kw["accum_out"] = accum_out
        return P.add("act", lambda e: e.activation(out=out, in_=in_, func=func, **kw), reads, writes)

    def dve(fn, reads, writes):
        return P.add("dve", fn, reads, writes)

    def tt(out, in0, in1, op, reads, writes, eng="dve"):
        return P.add(eng, lambda e: e.tensor_tensor(out=out, in0=in0, in1=in1, op=op), reads, writes)

    def tcopy(out, in_, reads, writes, eng="dve"):
        return P.add(eng, lambda e: e.tensor_copy(out=out, in_=in_), reads, writes)

    def load_piece(W, r0, nkc, c0, ncol, wbuf=None):
        cnt["ring"] += 1
        slot = ring[cnt["ring"] % len(ring)]
        src = W[r0:r0 + nkc * 128, c0:c0 + ncol].rearrange("(k p) c -> p k c", p=128)
        dma("pool", slot.t[:, 0:nkc, 0:ncol], src, [] if wbuf is None else [wbuf], [slot.buf], slot.buf)
        return slot

    dma("pool", ident.t[:, :], ident_d[:, :], [], [ident.buf], ident.buf)
    dma("sp", params.t[:, :], params_d[:, :], [], [params.buf], params.buf)
    dma("sp", gfin.t[:, :], gfin_d.partition_broadcast(128), [], [gfin.buf], gfin.buf)
    dve(lambda e: e.memset(ones.t[:, :], 1.0), [], [ones.buf])
    act(small.t[:, 0:24], params.t[:, PG_LAM:PG_LAM + 24], AF.Exp, [params.buf], [small.buf], scale=-1.0)
    act(small.t[:, 24:48], small.t[:, 0:24], AF.Ln, [small.buf], [small.buf], bias=1.0)
    dve(lambda e: e.tensor_scalar(out=cvec.t[:, :], in0=small.t[:, 24:48], scalar1=-8.0, scalar2=None,
                                  op0=ALU.mult), [small.buf], [cvec.buf])
    dve(lambda e: e.tensor_scalar(out=cvec2.t[:, :], in0=small.t[:, 24:48], scalar1=-16.0, scalar2=None,
                                  op0=ALU.mult), [small.buf], [cvec2.buf])

    ssbufs = [Buf('ss%d' % i) for i in range(8)]

    def norm_p1(src_ap, src_buf, hb, ss_col):
        ssap = small.t[:, 48 + ss_col:49 + ss_col]
        rsap = small.t[:, 56 + ss_col:57 + ss_col]
        sb = ssbufs[ss_col]
        act(hb.t[:, :], src_ap, AF.Square, [src_buf], [hb.buf, sb], accum_out=ssap)
        act(ssap, ssap, AF.Sqrt, [sb], [sb], scale=1.0 / D, bias=EPS)
        dve(lambda e: e.reciprocal(out=rsap, in_=ssap), [sb], [sb])
        dve(lambda e: e.tensor_scalar(out=hb.t[:, :], in0=src_ap, scalar1=rsap, scalar2=None, op0=ALU.mult),
            [src_buf, sb], [hb.buf])

    def norm_p2(hb, dst, tok0, gcol):
        for half in range(2):
            pb = next_psb()
            for i in range(8):
                kc = half * 8 + i
                tr(pb.t[:, i * 128:(i + 1) * 128], hb.t[:, kc * 128:(kc + 1) * 128], [hb.buf], [pb.buf])
            g = params.t[:, gcol + half * 8:gcol + half * 8 + 8].unsqueeze(2).to_broadcast([128, 8, 128])
            o = dst.t[:, half * 8:half * 8 + 8, tok0:tok0 + 128]
            i0 = pb.t[:, :].rearrange("p (k t) -> p k t", k=8)
            tt(o, i0, g, ALU.mult, [pb.buf, params.buf], [dst.buf])

    def norm_block(src_ap, src_buf, hb, ss_col, dst, tok0, gcol):
        norm_p1(src_ap, src_buf, hb, ss_col)
        norm_p2(hb, dst, tok0, gcol)

    def gemm_fm(acts, W, r0, KC, c0, ncols, T, epilogue, hook=None, wbuf=None):
        for pc in range(0, ncols, 512):
            w = min(512, ncols - pc)
            slot = load_piece(W, r0, KC, c0 + pc, w, wbuf)
            for ch in range(w // 128):
                for si, (afn, abuf) in enumerate(acts):
                    pb = next_psf()
                    for kc in range(KC):
                        mm(pb.t[:, 0:T], slot.t[:, kc, ch * 128:(ch + 1) * 128], afn(kc),
                           kc == 0, kc == KC - 1, [slot.buf, abuf], [pb.buf])
                    epilogue(pc // 128 + ch, si, pb)
                    if hook is not None:
                        hook()

    cur[0] = phase_base
    zero = S_("zero", [128, 4096], BF16)
    xrow = [S_("xrow%d" % i, [128, D], F32) for i in range(2)]
    hbs = [S_("hb%d" % i, [128, D], BF16) for i in range(4)]
    hTs = [S_("hT%d" % i, [128, 16, NT], BF16) for i in range(4)]
    stg = [S_("stg%d" % i, [128, 4, NT], BF16) for i in range(3)]
    memT = S_("memT", [128, 16, 256], BF16)
    assert cur[0] <= SB_END
    cnt.update({"xrow": 0, "hb": 0, "stg": 0})

    def next_(lst, key):
        cnt[key] += 1
        return lst[cnt[key] % len(lst)]

    dve(lambda e: e.memset(zero.t[:, :], 0.0), [], [zero.buf])
    zb = Buf("zfill")
    zops = []
    zv = zero.t[:, 0:4096].rearrange("p (h c) -> p h c", h=4)
    for i in range(3):
        zops.append(dma("sp", KT[4 * i:4 * i + 4, :, 0:1024].rearrange("h p c -> p h c"), zv, [zero.buf], [bKT], zb))
    zv2 = zero.t[:, 0:4096].rearrange("p (a c) -> p a c", a=8)
    for i in range(3):
        zops.append(dma("sp", V[0:1024, i * 512:(i + 1) * 512].rearrange("(a p) c -> p a c", p=128), zv2,
                        [zero.buf], [bV], zb))
    zv3 = zero.t[:, 0:384].rearrange("p (n c) -> p n c", n=12)
    zops.append(dma("sp", XBT[:, :, 0:32].rearrange("n p c -> p n c"), zv3, [zero.buf], [bXBT], zb))
    zops.append(dma("sp", XBT[:, :, 4128:4160].rearrange("n p c -> p n c"), zv3, [zero.buf], [bXBT], zb))
    for o in zops:
        o.semval = zb.semcnt

    _ckpt(0)
    def phase_M():
        for mb in range(2):
            xr = next_(xrow, "xrow")
            dma("sp", xr.t[:, :], mem[mb * 128:(mb + 1) * 128, :], [], [xr.buf], xr.buf)
            norm_block(xr.t[:, :], xr.buf, next_(hbs, "hb"), mb, memT, mb * 128, PG_MEM)

        def ep_kc(chunk, si, pb):
            P.add("act", lambda e: e.copy(out=kcT.t[:, chunk, :], in_=pb.t[:, 0:256]), [pb.buf], [kcT.buf])

        gemm_fm([(lambda kc: memT.t[:, kc, :], memT.buf)], w_mem_kv, 0, 16, 0, 1024, 256, ep_kc)
        for pc in range(2):
            slot = load_piece(w_mem_kv, 0, 16, 1024 + pc * 512, 512)
            for mb in range(2):
                pb = next_psf()
                for kc in range(16):
                    mm(pb.t[:, :], memT.t[:, kc, mb * 128:(mb + 1) * 128], slot.t[:, kc, :], kc == 0, kc == 15,
                       [memT.buf, slot.buf], [pb.buf])
                tcopy(vc.t[:, mb, pc * 512:(pc + 1) * 512], pb.t[:, :], [pb.buf], [vc.buf])


    _ckpt(1)
    QSCALE = 1.0 / math.sqrt(128.0)
    cnt["hT"] = 0
    def prep_macro(mt):
        hts = [next_(hTs, "hT"), next_(hTs, "hT")]
        p1s, p2s = [], []
        k = 0
        for si, j in enumerate((2 * mt, 2 * mt + 1)):
            hT = hts[si]
            for b in range(4):
                st_ = {}

                def t1(j=j, b=b, st_=st_, k=k):
                    xr = next_(xrow, "xrow")
                    hb = next_(hbs, "hb")
                    st_["hb"] = hb
                    r0 = j * NT + b * 128
                    dma("sp", xr.t[:, :], xs[r0:r0 + 128, :], [], [xr.buf], xr.buf)
                    norm_p1(xr.t[:, :], xr.buf, hb, 2 + (k % 4))

                def t2(j=j, b=b, hT=hT, st_=st_):
                    norm_p2(st_["hb"], hT, b * 128, PG_MIX)
                    if b == 3 and j < 4:
                        dma("sp", HT[j], hT.t[:, :, :].rearrange("p k t -> p (k t)"), [hT.buf], [bHT], hT.buf)
                p1s.append(t1)
                p2s.append(t2)
                k += 1
        thunks = [p1s[0]]
        for i in range(1, 8):
            thunks += [p1s[i], p2s[i - 1]]
        thunks.append(p2s[7])
        return hts, thunks

    hts_next, th0 = prep_macro(0)
    for th in th0:
        th()
    for mt in range(4):
        tiles = [2 * mt, 2 * mt + 1]
        hts = hts_next
        acts = [((lambda kc, h=h: h.t[:, kc, :]), h.buf) for h in hts]

        def fm_to_dram(c0, nchunks, dst, dbuf, col0, kind, hook=None):
            state = {}

            def ep(chunk, si, pb):
                key = (chunk // 4, si)
                if key not in state:
                    state[key] = next_(stg, "stg")
                st = state[key]
                o = st.t[:, chunk % 4, :]
                if kind == "q":
                    P.add("act", lambda e: e.mul(out=o, in_=pb.t[:, :], mul=QSCALE), [pb.buf], [st.buf])
                elif kind == "gelu":
                    act(o, pb.t[:, :], AF.Gelu_apprx_tanh, [pb.buf], [st.buf])
                elif alt():
                    P.add("act", lambda e: e.copy(out=o, in_=pb.t[:, :]), [pb.buf], [st.buf])
                else:
                    tcopy(o, pb.t[:, :], [pb.buf], [st.buf])
                if chunk % 4 == 3:
                    g0 = chunk - 3
                    t0 = col0 + tiles[si] * NT
                    dma("sp", dst[g0:g0 + 4, :, t0:t0 + NT].rearrange("h p t -> p h t"), st.t[:, :, :],
                        [st.buf], [dbuf], st.buf)
            gemm_fm(acts, w_in, 0, 16, c0, nchunks * 128, NT, ep, hook)

        def v_to_dram():
            for pc in range(3):
                slot = load_piece(w_in, 0, 16, 3072 + pc * 512, 512)
                for si, h in enumerate(hts):
                    st = next_(stg, "stg")
                    for ts in range(4):
                        pb = next_psf()
                        for kc in range(16):
                            mm(pb.t[:, :], h.t[:, kc, ts * 128:(ts + 1) * 128], slot.t[:, kc, :], kc == 0, kc == 15,
                               [h.buf, slot.buf], [pb.buf])
                        if alt():
                            P.add("act", lambda e, st=st, ts=ts, pb=pb: e.copy(out=st.t[:, ts, :], in_=pb.t[:, :]),
                                  [pb.buf], [st.buf])
                        else:
                            tcopy(st.t[:, ts, :], pb.t[:, :], [pb.buf], [st.buf])
                    r0 = 1024 + tiles[si] * NT
                    dma("sp", V[r0:r0 + NT, pc * 512:(pc + 1) * 512].rearrange("(a p) c -> p a c", p=128),
                        st.t[:, :, :], [st.buf], [bV], st.buf)

        if mt < 2:
            fm_to_dram(0, 12, QT, bQT, 0, "q")
        if mt < 3:
            fm_to_dram(1536, 12, KT, bKT, 1024, "copy")
            v_to_dram()
        if mt < 2:
            fm_to_dram(6144, 12, GT, bGT, 0, "gelu")
        if mt == 0:
            phase_M()
        hook = None
        if mt < 3:
            hts_next, pend_th = prep_macro(mt + 1)
            hk = {"n": 0}

            def hook(pend_th=pend_th, hk=hk):
                hk["n"] += 1
                if hk["n"] % 3 != 0 and pend_th:
                    pend_th.pop(0)()
        fm_to_dram(4608, 12, XBT, bXBT, 32, "copy", hook)
        if mt < 3:
            while pend_th:
                pend_th.pop(0)()

    _ckpt(2)
    cur[0] = phase_base
    XBs = [S_("XB", [128, 4160], BF16)]
    XB1s = [S_("XB1", [128, 4160], BF16)]
    Gcs = [S_("Gc", [128, OWN], BF16)]
    xc32s = [S_("xc32", [128, S], F32)]
    xcbs = [S_("xcb", [128, S], BF16)]
    Ycs = [S_("Yc%d" % i, [128, OWN], BF16) for i in range(2)]
    wabs = [S_("wab%d" % i, [128, 2, 2, 128], BF16) for i in range(2)]
    diags = [S_("diag%d" % i, [128, 5, 128], BF16) for i in range(2)]
    AFb = S_("AFb", [128, OWN], F32)
    BFb = S_("BFb", [128, OWN], F32)
    identf = S_("identf", [128, 128], F32)
    Abuf = S_("Abuf", [128, S], F32)
    Bbuf = S_("Bbuf", [128, S], F32)
    tmps = [S_("ltmp%d" % i, [128, 1024], F32) for i in range(8)]
    assert cur[0] <= SB_END
    ro = ring_off
    xc32s.append(S_("xc32b", [128, S], F32, at=ro))
    xcbs.append(S_("xcbb", [128, S], BF16, at=ro + 16384))
    XBs.append(S_("XBb", [128, 4160], BF16, at=ro + 24576))
    XB1s.append(S_("XB1b", [128, 4160], BF16, at=ro + 24576 + 8320))
    Gcs.append(S_("Gcb", [128, OWN], BF16, at=ro + 24576 + 2 * 8320))
    assert 24576 + 2 * 8320 + 4096 <= NRING * 16384
    ring_alias = [xc32s[1].buf, xcbs[1].buf, XBs[1].buf, XB1s[1].buf, Gcs[1].buf]
    barrier_bufs = [zero.buf] + [t.buf for t in xrow + hbs + hTs + stg] + [memT.buf]
    lbufs = [t.buf for t in [XBs[0], XB1s[0], Gcs[0], xc32s[0], xcbs[0], AFb, BFb, identf, Abuf, Bbuf] +
             Ycs + wabs + diags + tmps]

    def alias_barrier(old_bufs, new_bufs):
        for nb in new_bufs:
            for ob in old_bufs:
                for k, p in ob.readers.items():
                    q = nb.readers.get(k)
                    if q is None or q.n < p.n:
                        nb.readers[k] = p
                for k, p in ob.writers.items():
                    q = nb.readers.get(k)
                    if q is None or q.n < p.n:
                        nb.readers[k] = p

    alias_barrier(barrier_bufs, lbufs)
    alias_barrier([r.buf for r in ring], ring_alias)
    dma("sp", identf.t[:, :], ident_d[:, :], [], [identf.buf], identf.buf)
    cnt["ltmp"] = 0

    def chunk_start(n):
        k = n % 2
        XB, XB1, Gc, wab, diag, xc32, xcb = XBs[k], XB1s[k], Gcs[k], wabs[k], diags[k], xc32s[k], xcbs[k]
        dma("sp", XB.t[:, :], XBT[n], [bXBT], [XB.buf], XB.buf)
        tcopy(XB1.t[:, 0:4158], XB.t[:, 1:4159], [XB.buf], [XB1.buf])
        dma("sp", Gc.t[:, :], GT[n], [bGT], [Gc.buf], Gc.buf)
        dma("pool", wab.t[:, :, :, :], lruw_d[:, :, n].rearrange("r k c d -> c r k d"), [], [wab.buf], wab.buf)
        for o in range(5):
            tap = params.t[:, PG_TAP + n * 5 + o:PG_TAP + n * 5 + o + 1]
            dve(lambda e, o=o, tap=tap: e.tensor_scalar(out=diag.t[:, o, :], in0=identf.t[:, :], scalar1=tap,
                                                         scalar2=None, op0=ALU.mult),
                [identf.buf, params.buf], [diag.buf])
        cb = params.t[:, PG_CB + n:PG_CB + n + 1]
        for cg in range(4):
            w_, wb_ = next_psw()
            for half in range(2):
                st = cg * 2 + half
                for o in range(5):
                    src = XB if o % 2 == 0 else XB1
                    c0 = st * 512 + 30 + (o if o % 2 == 0 else o - 1)
                    mm(w_[:, half * 512:(half + 1) * 512], diag.t[:, o, :], src.t[:, c0:c0 + 512], o == 0, o == 4,
                       [diag.buf, src.buf], [wb_[half]])
            sl = slice(cg * 1024, (cg + 1) * 1024)
            dve(lambda e, sl=sl, w_=w_, cb=cb: e.tensor_scalar(out=xc32.t[:, sl], in0=w_[:, :], scalar1=cb, scalar2=None,
                                                               op0=ALU.add), wb_ + [params.buf], [xc32.buf])
            tcopy(xcb.t[:, sl], xc32.t[:, sl], [xc32.buf], [xcb.buf])

    def lru_dir(n, role, ngr, Adst, Bdst, start_col, abuf, bbuf):
        k = n % 2
        wab, xc32, xcb = wabs[k], xc32s[k], xcbs[k]
        ba = params.t[:, PG_BA + role * 12 + n:PG_BA + role * 12 + n + 1]
        bi = params.t[:, PG_BI + role * 12 + n:PG_BI + role * 12 + n + 1]
        cv = cvec.t[:, role * 12 + n:role * 12 + n + 1]
        pend = []

        def stage2(gi, sl, te2, tb):
            act(te2.t[:, :], te2.t[:, :], AF.Sqrt, [te2.buf], [te2.buf], scale=-1.0, bias=1.0)
            tt(Bdst(sl), tb.t[:, :], te2.t[:, :], ALU.mult, [tb.buf, te2.buf], [bbuf])
            if start_col // 1024 == gi:
                c = start_col % 1024
                tcopy(Bdst(slice(start_col, start_col + 1)), tb.t[:, c:c + 1], [tb.buf], [bbuf])

        for gi in range(ngr):
            sl = slice(gi * 1024, (gi + 1) * 1024)
            wr, wrb = next_psw()
            for half in range(2):
                hs = slice(gi * 1024 + half * 512, gi * 1024 + (half + 1) * 512)
                mm(wr[:, half * 512:(half + 1) * 512], wab.t[:, role, 0, :], xcb.t[:, hs], True, True,
                   [wab.buf, xcb.buf], [wrb[half]])
            wi, wib = next_psw()
            for half in range(2):
                hs = slice(gi * 1024 + half * 512, gi * 1024 + (half + 1) * 512)
                mm(wi[:, half * 512:(half + 1) * 512], wab.t[:, role, 1, :], xcb.t[:, hs], True, True,
                   [wab.buf, xcb.buf], [wib[half]])
            tr_ = next_(tmps, "ltmp")
            ti = next_(tmps, "ltmp")
            te2 = next_(tmps, "ltmp")
            tb = next_(tmps, "ltmp")
            act(tr_.t[:, :], wr[:, :], AF.Sigmoid, wrb + [params.buf], [tr_.buf], bias=ba)
            act(ti.t[:, :], wi[:, :], AF.Sigmoid, wib + [params.buf], [ti.buf], bias=bi)
            a_ap = Adst(sl)
            act(a_ap, tr_.t[:, :], AF.Exp, [tr_.buf, cvec.buf], [abuf], scale=cv)
            tt(te2.t[:, :], a_ap, a_ap, ALU.mult, [abuf], [te2.buf])
            tt(tb.t[:, :], ti.t[:, :], xc32.t[:, sl], ALU.mult, [ti.buf, xc32.buf], [tb.buf])
            if pend:
                stage2(*pend.pop())
            pend.append((gi, sl, te2, tb))
        stage2(*pend.pop())

    conv_th = []
    for i in range(16):
        conv_th.append(lambda i=i: dma("pool", WUB[i * 128:(i + 1) * 128, :], w_up[i * 128:(i + 1) * 128, :],
                                       [], [bWUB], bWUB))
    for i in range(16):
        conv_th.append(lambda i=i: dma("pool", WDB[i * 512:(i + 1) * 512, :].rearrange("(a p) c -> p a c", p=128),
                                       w_down[i * 512:(i + 1) * 512, :].rearrange("(a p) c -> p a c", p=128),
                                       [], [bWDB], bWDB))
    if LCHUNKS > 0:
        chunk_start(0)
    for n in range(LCHUNKS):
        k = n % 2
        xc32, Gc, Yc = xc32s[k], Gcs[k], Ycs[k]
        lru_dir(n, 0, 2, lambda sl: AFb.t[:, sl], lambda sl: BFb.t[:, sl], 0, AFb.buf, BFb.buf)
        dve(lambda e: e.tensor_tensor_scan(out=BFb.t[:, :], data0=AFb.t[:, :], data1=BFb.t[:, :],
                                           initial=0.0, op0=ALU.mult, op1=ALU.add), [AFb.buf, BFb.buf], [BFb.buf])
        lru_dir(n, 1, 4, lambda sl: Abuf.t[:, sl], lambda sl: Bbuf.t[:, sl], 4095, Abuf.buf, Bbuf.buf)
        if n + 1 < LCHUNKS:
            chunk_start(n + 1)
        for _ in range(3):
            if conv_th:
                conv_th.pop(0)()
        dve(lambda e, xc32=xc32: e.tensor_tensor_scan(out=xc32.t[:, ::-1], data0=Abuf.t[:, ::-1],
                                                      data1=Bbuf.t[:, ::-1], initial=0.0, op0=ALU.mult,
                                                      op1=ALU.add),
            [Abuf.buf, Bbuf.buf], [xc32.buf])
        tt(BFb.t[:, :], xc32.t[:, 0:OWN], BFb.t[:, :], ALU.add, [xc32.buf, BFb.buf], [BFb.buf])
        tt(Yc.t[:, :], BFb.t[:, :], Gc.t[:, :], ALU.mult, [BFb.buf, Gc.buf], [Yc.buf])
        dma("sp", YLT[n], Yc.t[:, :], [Yc.buf], [bYLT], Yc.buf)
    while conv_th:
        conv_th.pop(0)()
    alias_barrier(ring_alias, [r.buf for r in ring])
    lbufs = lbufs + ring_alias

    _ckpt(3)
    cur[0] = phase_base
    a_base = cur[0]
    YA = S_("YA", [128, OWN], BF16)
    tabs = S_("tabs", [128, 36, 128], BF16)
    Qh = [S_("Qh%d" % i, [128, OWN], BF16) for i in range(2)]
    Kh = [S_("Kh%d" % i, [128, 4096], BF16) for i in range(2)]
    Vh = [S_("Vh%d" % i, [128, 32, 128], BF16) for i in range(2)]
    acc = S_("acc", [128, 2, OWN], F32)
    Pt = [S_("Pt%d" % i, [128, 2, 128], BF16) for i in range(5)]
    assert cur[0] <= SB_END
    abufs = [YA.buf, acc.buf, tabs.buf] + [t.buf for t in Qh + Kh + Vh + Pt]
    alias_barrier(lbufs, abufs)
    cnt.update({"Qh": 0, "Kh": 0, "Vh": 0, "Pt": 0})
    dma("pool", tabs.t[:, :, :], tabs_d.rearrange("t k q -> k t q"), [], [tabs.buf], tabs.buf)

    for hh in range(4):
        for g, d in enumerate((1, 4, 16)):
            head = 4 * g + hh
            nb = 16 // d
            nkt = nb + 1
            q_t = next_(Qh, "Qh")
            k_t = next_(Kh, "Kh")
            v_t = next_(Vh, "Vh")
            dma("sp", q_t.t[:, :], QT[head], [bQT], [q_t.buf], q_t.buf)
            c_lo = 1024 - 64 * d
            c_hi = 1024 + OWN + 64 * d
            dma("sp", k_t.t[:, c_lo:c_hi], KT[head, :, c_lo:c_hi], [bKT], [k_t.buf], k_t.buf)
            nrow = d * 128 * nkt
            for r in range(d):
                src = V[c_lo + r:c_lo + nrow:d, head * 128:(head + 1) * 128].rearrange("(m p) c -> p m c", p=128)
                dma("sp", v_t.t[:, r * nkt:(r + 1) * nkt, :], src, [bV], [v_t.buf], v_t.buf)
            def scores(r, n):
                q0 = r + d * 128 * n
                qs = slice(q0, q0 + d * 127 + 1, d)
                ps = next_psf()
                for mi in range(2):
                    m = n + mi
                    k0 = c_lo + r + 128 * d * m
                    ks = slice(k0, k0 + 127 * d + 1, d)
                    variant = 2 if mi == 1 else (0 if n == 0 else 1)
                    mm(ps.t[:, mi * 128:(mi + 1) * 128], k_t.t[:, ks], q_t.t[:, qs], True, False,
                       [k_t.buf, q_t.buf], [ps.buf])
                    mm(ps.t[:, mi * 128:(mi + 1) * 128], ident.t[:, :], tabs.t[:, head * 3 + variant, :], False,
                       True, [ident.buf, tabs.buf], [ps.buf])
                pt = next_(Pt, "Pt")
                act(pt.t[:, :, :], ps.t[:, 0:256].rearrange("p (a q) -> p a q", a=2), AF.Exp, [ps.buf], [pt.buf])
                return (r, n, qs, pt)

            def pv(r, n, qs, pt):
                p2 = next_psf()
                for mi in range(2):
                    m = n + mi
                    mm(p2.t[:, 0:128], v_t.t[:, r * nkt + m, :], pt.t[:, mi, :], mi == 0, mi == 1,
                       [v_t.buf, pt.buf], [p2.buf])
                for mi in range(2):
                    mm(p2.t[:, 128:256], ones.t[:, :], pt.t[:, mi, :], mi == 0, mi == 1,
                       [ones.buf, pt.buf], [p2.buf])
                o = acc.t[:, :, qs]
                i0_ = p2.t[:, 0:256].rearrange("p (a q) -> p a q", a=2)
                if g == 0:
                    tcopy(o, i0_, [p2.buf], [acc.buf])
                else:
                    tt(o, i0_, o, ALU.add, [p2.buf, acc.buf], [acc.buf])

            pend = []
            for r in range(d):
                for n in range(nb):
                    pend.append(scores(r, n))
                    if len(pend) > 2:
                        pv(*pend.pop(0))
            while pend:
                pv(*pend.pop(0))
        dve(lambda e: e.reciprocal(out=acc.t[:, 1, :], in_=acc.t[:, 1, :]), [acc.buf], [acc.buf])
        tt(YA.t[:, :], acc.t[:, 0, :], acc.t[:, 1, :], ALU.mult, [acc.buf], [YA.buf])
        dma("sp", YAT[hh], YA.t[:, :], [YA.buf], [bYAT], YA.buf)


    _ckpt(4)
    cur[0] = a_base
    xm = S_("xm", [128, 4, D], F32)
    hT16 = S_("hT16", [128, 16, NT], BF16)
    R32 = S_("R32", [128, 16384], BF16)
    mixT = S_("mixT", [128, 16, NT], BF16)
    mixf = S_("mixf", [128, 4, NT], F32)
    gt_ = [S_("gt%d" % i, [128, NT], F32) for i in range(2)]
    tf_ = [S_("tf%d" % i, [128, NT], F32) for i in range(2)]
    hb2 = [S_("hbm%d" % i, [128, D], BF16) for i in range(1)]
    yat = S_("yat", [128, 4, NT], BF16)
    ring.append(S_("ring3", [128, 16, 512], BF16))
    assert cur[0] <= SB_END, cur[0]
    b_yl, b_yc, b_qc, b_Pc = Buf('r_yl'), Buf('r_yc'), Buf('r_qc'), Buf('r_Pc')
    RBA = [b_yl, b_yc, b_qc, b_Pc]
    gbufs = [xm.buf, hT16.buf, b_yl, b_yc, b_qc, b_Pc, mixT.buf, mixf.buf, yat.buf, ring[3].buf] + [t.buf for t in gt_ + tf_ + hb2]
    alias_barrier(abufs, gbufs)
    cnt.update({"gt": 0, "tf": 0, "hbm": 0})
    yl = R32.t[:, 0:6144].rearrange("p (n t) -> p n t", n=12)
    yc = R32.t[:, 6144:10240].rearrange("p (n t) -> p n t", n=8)
    qc = R32.t[:, 10240:14336].rearrange("p (n t) -> p n t", n=8)
    Pc = R32.t[:, 14336:16384].rearrange("p (n t) -> p n t", n=4)
    uT = R32.t[:, :].rearrange("p (n t) -> p n t", n=32)

    for j in range(4):
        t0 = j * NT
        if j == 0:
            dma("sp", hT16.t[:, :, :].rearrange("p k t -> p (k t)"), HT[j], [bHT], [hT16.buf], hT16.buf)
        dma("sp", yat.t[:, :, :], YAT[:, :, t0:t0 + NT].rearrange("h p t -> p h t"), [bYAT], [yat.buf], yat.buf)
        dma("sp", yl, YLT[:, :, t0:t0 + NT].rearrange("n p t -> p n t"), [bYLT], [b_yl], b_yl)
        dma("sp", xm.t[:, :, :], xs[t0:t0 + NT, :].rearrange("(a p) c -> p a c", p=128), [], [xm.buf], xm.buf)

        def ep_qc(chunk, si, pb):
            P.add("act", lambda e: e.mul(out=qc[:, chunk, :], in_=pb.t[:, :], mul=1.0 / 16.0), [pb.buf], [b_qc])
        gemm_fm([(lambda kc: hT16.t[:, kc, :], hT16.buf)], w_in, 0, 16, 7680, 1024, NT, ep_qc)

        def xa_s(h):
            for mc in range(2):
                pb = next_psf()
                for cc in range(2):
                    mm(pb.t[:, :], kcT.t[:, h * 2 + cc, mc * 128:(mc + 1) * 128], qc[:, h * 2 + cc, :], cc == 0, cc == 1,
                       [kcT.buf, b_qc], [pb.buf])
                act(Pc[:, (h % 2) * 2 + mc, :], pb.t[:, :], AF.Exp, [pb.buf], [b_Pc])

        def xa_p(h):
            pd = next_psf()
            for mc in range(2):
                mm(pd.t[:, :], ones.t[:, :], Pc[:, (h % 2) * 2 + mc, :], mc == 0, mc == 1, [ones.buf, b_Pc], [pd.buf])
            rec = next_(tf_, "tf")
            dve(lambda e, rec=rec, pd=pd: e.reciprocal(out=rec.t[:, :], in_=pd.t[:, :]), [pd.buf], [rec.buf])
            for cc in range(2):
                pb = next_psf()
                for mc in range(2):
                    mm(pb.t[:, :], vc.t[:, mc, h * 256 + cc * 128:h * 256 + (cc + 1) * 128], Pc[:, (h % 2) * 2 + mc, :],
                       mc == 0, mc == 1, [vc.buf, b_Pc], [pb.buf])
                tt(yc[:, h * 2 + cc, :], pb.t[:, :], rec.t[:, :], ALU.mult, [pb.buf, rec.buf], [b_yc])

        xa_thunks = [lambda: xa_s(0), lambda: xa_s(1), lambda: xa_p(0), lambda: xa_s(2), lambda: xa_p(1),
                     lambda: xa_s(3), lambda: xa_p(2), lambda: xa_p(3)]

        branches = [
            (w_o_attn, 4, lambda kc: yat.t[:, kc, :], yat.buf),
            (w_o_lru, 12, lambda kc: yl[:, kc, :], b_yl),
            (w_o_mem, 8, lambda kc: yc[:, kc, :], b_yc),
        ]
        for fb in range(4):
            for br, (Wb, KCb, afn, abuf) in enumerate(branches):
                gslot = load_piece(w_gate, 0, 16, br * D + fb * 512, 512)
                oslot = load_piece(Wb, 0, KCb, fb * 512, 512)
                if br == 2:
                    while xa_thunks:
                        xa_thunks.pop(0)()
                for ch in range(4):
                    fcg = fb * 4 + ch
                    if xa_thunks:
                        xa_thunks.pop(0)()
                    pg = next_psf()
                    for kc in range(16):
                        mm(pg.t[:, :], gslot.t[:, kc, ch * 128:(ch + 1) * 128], hT16.t[:, kc, :], kc == 0, kc == 15,
                           [gslot.buf, hT16.buf], [pg.buf])
                    g_ = next_(gt_, "gt")
                    bg = params.t[:, PG_BG + br * 16 + fcg:PG_BG + br * 16 + fcg + 1]
                    act(g_.t[:, :], pg.t[:, :], AF.Sigmoid, [pg.buf, params.buf], [g_.buf], bias=bg)
                    po = next_psf()
                    for kc in range(KCb):
                        mm(po.t[:, :], oslot.t[:, kc, ch * 128:(ch + 1) * 128], afn(kc), kc == 0, kc == KCb - 1,
                           [oslot.buf, abuf], [po.buf])
                    if br == 0:
                        tt(mixf.t[:, ch, :], po.t[:, :], g_.t[:, :], ALU.mult, [po.buf, g_.buf], [mixf.buf])
                    else:
                        tf = next_(tf_, "tf")
                        tt(tf.t[:, :], po.t[:, :], g_.t[:, :], ALU.mult, [po.buf, g_.buf], [tf.buf])
                        if br == 1:
                            tt(mixf.t[:, ch, :], mixf.t[:, ch, :], tf.t[:, :], ALU.add, [mixf.buf, tf.buf], [mixf.buf])
                        else:
                            tt(mixT.t[:, fcg, :], mixf.t[:, ch, :], tf.t[:, :], ALU.add, [mixf.buf, tf.buf],
                               [mixT.buf])

        for cb_ in range(4):
            slot = load_piece(w_out, 0, 16, cb_ * 512, 512)
            cs = slice(cb_ * 512, (cb_ + 1) * 512)
            for ts in range(4):
                pb = next_psf()
                for kc in range(16):
                    mm(pb.t[:, :], mixT.t[:, kc, ts * 128:(ts + 1) * 128], slot.t[:, kc, :], kc == 0, kc == 15,
                       [mixT.buf, slot.buf], [pb.buf])
                tt(xm.t[:, ts, cs], pb.t[:, :], xm.t[:, ts, cs], ALU.add, [pb.buf, xm.buf], [xm.buf])
        if DEBUG:
            dma("sp", DBG2[t0:t0 + NT, :].rearrange("(a p) c -> p a c", p=128), xm.t[:, :, :], [xm.buf], [bDBG],
                xm.buf)

        for ts in range(4):
            norm_block(xm.t[:, ts, :], xm.buf, next_(hb2, "hbm"), 2 + ts, hT16, ts * 128, PG_MLP)
        for fh in range(2):
            def ep_up(chunk, si, pb):
                tf = next_(tf_, "tf")
                act(tf.t[:, :], pb.t[:, :], AF.Relu, [pb.buf], [tf.buf])
                tt(uT[:, chunk, :], tf.t[:, :], tf.t[:, :], ALU.mult, [tf.buf], RBA)
            gemm_fm([(lambda kc: hT16.t[:, kc, :], hT16.buf)], WUB, 0, 16, fh * 4096, 4096, NT, ep_up, wbuf=bWUB)
            if fh == 1 and j < 3:
                dma("sp", hT16.t[:, :, :].rearrange("p k t -> p (k t)"), HT[j + 1], [bHT], [hT16.buf], hT16.buf)
            for cb_ in range(4):
                cs = slice(cb_ * 512, (cb_ + 1) * 512)
                pbs = [next_psf() for _ in range(4)]
                for kp in range(2):
                    slot = load_piece(WDB, (fh * 32 + kp * 16) * 128, 16, cb_ * 512, 512, bWDB)
                    for ts in range(4):
                        for kc in range(16):
                            mm(pbs[ts].t[:, :], uT[:, kp * 16 + kc, ts * 128:(ts + 1) * 128], slot.t[:, kc, :],
                               kp == 0 and kc == 0, kp == 1 and kc == 15, RBA + [slot.buf], [pbs[ts].buf])
                for ts in range(4):
                    tt(xm.t[:, ts, cs], pbs[ts].t[:, :], xm.t[:, ts, cs], ALU.add, [pbs[ts].buf, xm.buf], [xm.buf])

        for ts in range(4):
            hb = next_(hb2, "hbm")
            ssap = small.t[:, 50 + ts:51 + ts]
            rsap = small.t[:, 58 + ts:59 + ts]
            sb = ssbufs[2 + ts]
            act(hb.t[:, :], xm.t[:, ts, :], AF.Square, [xm.buf], [hb.buf, sb], accum_out=ssap)
            act(ssap, ssap, AF.Sqrt, [sb], [sb], scale=1.0 / D, bias=EPS)
            dve(lambda e, rsap=rsap, ssap=ssap: e.reciprocal(out=rsap, in_=ssap), [sb], [sb])
            dve(lambda e, ts=ts, rsap=rsap: e.scalar_tensor_tensor(out=xm.t[:, ts, :], in0=xm.t[:, ts, :], scalar=rsap,
                                                                   in1=gfin.t[:, :], op0=ALU.mult, op1=ALU.mult),
                [xm.buf, sb, gfin.buf], [xm.buf])
        st_op = dma("sp", y[t0:t0 + NT, :].rearrange("(a p) c -> p a c", p=128), xm.t[:, :, :], [xm.buf], [], xm.buf)

    P.add("sp", None, [xm.buf], [xm.buf])


def _t5_bucket(rel):
    nb = 16
    max_exact = 8
    sign = (rel > 0).astype(np.int32) * nb
    n = np.abs(rel)
    large = max_exact + (np.log(np.maximum(n, 1) / max_exact) / np.log(1024 / max_exact) * (nb - max_exact)).astype(np.int32)
    large = np.minimum(large, nb - 1)
    return (sign + np.where(n < max_exact, n, large)).astype(np.int32)


def _tables(rel_bias, par):
    tabs = np.empty((36, 128, 128), np.float32)
    k = np.arange(128)[:, None]
    q = np.arange(128)[None, :]
    for g, d in enumerate((1, 4, 16)):
        for hh in range(4):
            head = 4 * g + hh
            for variant in range(3):
                delta = 64 if variant == 2 else -64
                rel = delta + k - q
                band = np.abs(rel) <= 64
                if variant == 0:
                    band = band & (k >= 64)
                relt = rel * d
                if par:
                    relt = -relt
                vals = rel_bias[_t5_bucket(relt), head]
                tabs[head * 3 + variant] = np.where(band, vals, np.float32(-30000.0))
    return tabs


def _col(v, n):
    return np.ascontiguousarray(v.reshape(n, 128).T)


def _prep(inputs):
    f = lambda k: np.asarray(inputs[k], dtype=np.float32)
    x, mem, rel_bias = f("x"), f("mem"), f("rel_bias")
    conv_w, conv_b = f("conv_w")[0], f("conv_b")[0]
    lru_wa, lru_wi = f("lru_wa")[0], f("lru_wi")[0]
    lru_ba, lru_bi, lru_lam = f("lru_ba")[0], f("lru_bi")[0], f("lru_lambda")[0]
    shared = {
        "gfin": f("norm_final").reshape(1, D),
        "ident": np.eye(128, dtype=np.float32),
        "w_in": f("w_in")[0], "w_gate": f("w_gate")[0], "w_mem_kv": f("w_mem_kv")[0],
        "w_o_attn": f("w_o_attn")[0], "w_o_lru": f("w_o_lru")[0], "w_o_mem": f("w_o_mem")[0],
        "w_out": f("w_out")[0], "w_up": f("w_up")[0], "w_down": f("w_down")[0],
    }
    per_par = []
    for par in range(2):
        roles = (0, 1) if par == 0 else (1, 0)
        params = np.zeros((128, 240), np.float32)
        params[:, 0:16] = _col(f("norm_mix")[0], 16)
        params[:, 16:32] = _col(f("norm_mem")[0], 16)
        params[:, 32:48] = _col(f("norm_mlp")[0], 16)
        params[:, 48:96] = _col(f("b_gate")[0], 48)
        taps = np.zeros((5, 1536), np.float32)
        if par == 0:
            taps[1:5] = conv_w
        else:
            taps[0:4] = conv_w[::-1]
        params[:, 96:156] = taps.reshape(5, 12, 128).transpose(2, 1, 0).reshape(128, 60)
        params[:, 156:168] = _col(conv_b, 12)
        for ri, dr in enumerate(roles):
            params[:, 168 + ri * 12:168 + ri * 12 + 12] = lru_ba[dr].T
            params[:, 192 + ri * 12:192 + ri * 12 + 12] = lru_bi[dr].T
            params[:, 216 + ri * 12:216 + ri * 12 + 12] = _col(lru_lam[dr], 12)
        lruw = np.stack([np.stack([lru_wa[dr], lru_wi[dr]]) for dr in roles])
        per_par.append({"params": params, "lruw": np.ascontiguousarray(lruw), "tabs": _tables(rel_bias, par)})

    in_maps = []
    for c in range(8):
        b, par = c // 2, c % 2
        m = dict(shared)
        m.update(per_par[par])
        m["xs"] = np.ascontiguousarray(x[b] if par == 0 else x[b, ::-1])
        m["mem"] = np.ascontiguousarray(mem[b])
        in_maps.append(m)
    return in_maps


def kernel(**inputs):
    in_maps = _prep(inputs)
    nc = build_nc()
    res = run_bass_kernel_spmd(nc, in_maps, core_ids=list(range(8)))
    out = np.empty((4, S, D), np.float32)
    for c in range(8):
        b, par = c // 2, c % 2
        yv = np.asarray(res.results[c]["y"], dtype=np.float32)
        if par == 0:
            out[b, 0:OWN] = yv
        else:
            out[b, OWN:S] = yv[::-1]
    if DEBUG:
        kernel.last = res
    return out
```
